# Optimizing a Trainium2 kernel written in Bass

```python
import math
import jax, jax.numpy as jnp
from jax import lax
import numpy as np

D_MODEL = 1024
BATCH = 8
SEQ = 4096
DEPTH = 2

N_A_LAYERS = DEPTH // 2
N_B_LAYERS = DEPTH - N_A_LAYERS
CONV_WIDTH = 3
N_HEADS = 8
HEAD_DIM = D_MODEL // (2 * N_HEADS)
V_HEAD_DIM = 2 * HEAD_DIM
QK_WIDTH = N_HEADS * 2 * HEAD_DIM
V_WIDTH = N_HEADS * V_HEAD_DIM
D_FF = ((8 * D_MODEL // 3 + 127) // 128) * 128
PLE_DIM = 256
N_BUCKETS = 32
MAX_DISTANCE = 128
Q_BLOCK = 128
EPS = 1e-6

kernel_name = "yoco_shortconv_diffattn_convffn_block"


def rms_norm(x, g):
    xf = x.astype(jnp.float32)
    y = xf * lax.rsqrt(jnp.mean(xf * xf, axis=-1, keepdims=True) + EPS)
    return (y * g.astype(jnp.float32)).astype(x.dtype)


def causal_dwconv(x, w):
    c = x.shape[-1]
    return lax.conv_general_dilated(
        x, w[:, None, :].astype(x.dtype), window_strides=(1,),
        padding=[(CONV_WIDTH - 1, 0)], dimension_numbers=('NWC', 'WIO', 'NWC'),
        feature_group_count=c)


def rel_bucket(dist):
    max_exact = N_BUCKETS // 2
    large = max_exact + (jnp.log(jnp.maximum(dist, 1).astype(jnp.float32) / max_exact)
                         / math.log(MAX_DISTANCE / max_exact)
                         * (N_BUCKETS - max_exact)).astype(jnp.int32)
    large = jnp.minimum(large, N_BUCKETS - 1)
    return jnp.where(dist < max_exact, dist, large)


def short_conv_mixer(h, w_in, w_conv, w_out):
    gb, gc, u = jnp.split(h @ w_in, 3, axis=-1)
    return (gb * causal_dwconv(gc * u, w_conv)) @ w_out


def conv_ffn(h, w_up, w_conv, w_down):
    gate, up = jnp.split(causal_dwconv(h @ w_up, w_conv), 2, axis=-1)
    return (jax.nn.silu(gate) * up) @ w_down


def per_layer_embed(x, p_i, g_pre, w_gate, w_proj, g_post):
    gate = jax.nn.sigmoid(rms_norm(x, g_pre) @ w_gate)
    return rms_norm(gate * (p_i @ w_proj), g_post)


def shared_kv(x, g_kv, w_kv):
    b, s, _ = x.shape
    k, v = jnp.split(rms_norm(x, g_kv) @ w_kv, [QK_WIDTH], axis=-1)
    return (k.reshape(b, s, N_HEADS, 2, HEAD_DIM),
            v.reshape(b, s, N_HEADS, V_HEAD_DIM))


def diff_attention(h, k, v, dist_bias, w_q, lam_vecs, subln_g, w_o, lambda_init):
    b, s, _ = h.shape
    nb = s // Q_BLOCK
    q = (h @ w_q).reshape(b, nb, Q_BLOCK, N_HEADS, 2, HEAD_DIM).transpose(1, 0, 2, 3, 4, 5)
    lv = lam_vecs.astype(jnp.float32)
    lam = jnp.exp(jnp.sum(lv[0] * lv[1])) - jnp.exp(jnp.sum(lv[2] * lv[3])) + lambda_init
    kf = k.astype(jnp.float32)
    vf = v.astype(jnp.float32)
    bias_tab = dist_bias.astype(jnp.float32)
    k_pos = jnp.arange(s)
    scale = HEAD_DIM ** -0.5

    def block(args):
        qb, blk = args
        q_pos = blk * Q_BLOCK + jnp.arange(Q_BLOCK)
        dist = q_pos[:, None] - k_pos[None, :]
        bias = jnp.take(bias_tab, jnp.maximum(dist, 0), axis=0).transpose(2, 0, 1)
        logits = jnp.einsum('bqhmd,bkhmd->bhmqk', qb.astype(jnp.float32), kf) * scale
        logits = logits + bias[None, :, None]
        logits = jnp.where((dist >= 0)[None, None, None], logits, -jnp.inf)
        probs = jax.nn.softmax(logits, axis=-1)
        attn = probs[:, :, 0] - lam * probs[:, :, 1]
        return jnp.einsum('bhqk,bkhe->bqhe', attn, vf)

    o = lax.map(block, (q, jnp.arange(nb)))
    o = o.transpose(1, 0, 2, 3, 4).reshape(b, s, N_HEADS, V_HEAD_DIM)
    o = rms_norm(o, subln_g) * (1.0 - lambda_init)
    return o.reshape(b, s, V_WIDTH).astype(h.dtype) @ w_o


def setup_inputs(seed: int = 0) -> dict:
    key = jax.random.key(seed)
    ks = jax.random.split(key, 24)
    f32 = jnp.float32

    def nrm(k, shape, scale):
        return jax.random.normal(k, shape, f32) * scale

    def gain(k, shape):
        return 1.0 + 0.05 * jax.random.normal(k, shape, f32)

    return {
        "x": nrm(ks[0], (BATCH, SEQ, D_MODEL), 1.0),
        "p": nrm(ks[1], (DEPTH, BATCH, SEQ, PLE_DIM), 1.0),
        "g_pre_mix": gain(ks[2], (DEPTH, D_MODEL)),
        "g_post_mix": gain(ks[3], (DEPTH, D_MODEL)),
        "g_pre_ffn": gain(ks[4], (DEPTH, D_MODEL)),
        "g_post_ffn": gain(ks[5], (DEPTH, D_MODEL)),
        "g_pre_ple": gain(ks[6], (DEPTH, D_MODEL)),
        "g_post_ple": gain(ks[7], (DEPTH, D_MODEL)),
        "w_sc_in": nrm(ks[8], (N_A_LAYERS, D_MODEL, 3 * D_MODEL), D_MODEL ** -0.5),
        "w_sc_conv": nrm(ks[9], (N_A_LAYERS, CONV_WIDTH, D_MODEL), CONV_WIDTH ** -0.5),
        "w_sc_out": nrm(ks[10], (N_A_LAYERS, D_MODEL, D_MODEL), D_MODEL ** -0.5),
        "g_kv": gain(ks[11], (D_MODEL,)),
        "w_kv": nrm(ks[12], (D_MODEL, QK_WIDTH + V_WIDTH), D_MODEL ** -0.5),
        "rel_bias": nrm(ks[13], (N_BUCKETS, N_HEADS), 0.5),
        "w_q": nrm(ks[14], (N_B_LAYERS, D_MODEL, QK_WIDTH), D_MODEL ** -0.5),
        "diff_lambda": nrm(ks[15], (N_B_LAYERS, 4, HEAD_DIM), 0.1),
        "g_subln": gain(ks[16], (N_B_LAYERS, V_HEAD_DIM)),
        "w_o": nrm(ks[17], (N_B_LAYERS, V_WIDTH, D_MODEL), V_WIDTH ** -0.5),
        "w_ffn_up": nrm(ks[18], (DEPTH, D_MODEL, 2 * D_FF), D_MODEL ** -0.5),
        "w_ffn_conv": nrm(ks[19], (DEPTH, CONV_WIDTH, 2 * D_FF), CONV_WIDTH ** -0.5),
        "w_ffn_down": nrm(ks[20], (DEPTH, D_FF, D_MODEL), D_FF ** -0.5),
        "w_ple_gate": nrm(ks[21], (DEPTH, D_MODEL, D_MODEL), D_MODEL ** -0.5),
        "w_ple_proj": nrm(ks[22], (DEPTH, PLE_DIM, D_MODEL), PLE_DIM ** -0.5),
    }


def reference(x, p, g_pre_mix, g_post_mix, g_pre_ffn, g_post_ffn, g_pre_ple, g_post_ple,
              w_sc_in, w_sc_conv, w_sc_out, g_kv, w_kv, rel_bias, w_q, diff_lambda,
              g_subln, w_o, w_ffn_up, w_ffn_conv, w_ffn_down, w_ple_gate, w_ple_proj):
    s = x.shape[1]
    dist_bias = jnp.take(rel_bias, rel_bucket(jnp.arange(s)), axis=0)
    k = None
    v = None
    for i in range(DEPTH):
        h = rms_norm(x, g_pre_mix[i])
        if i < N_A_LAYERS:
            mix = short_conv_mixer(h, w_sc_in[i], w_sc_conv[i], w_sc_out[i])
        else:
            j = i - N_A_LAYERS
            if j == 0:
                k, v = shared_kv(x, g_kv, w_kv)
            lambda_init = 0.8 - 0.6 * math.exp(-0.3 * i)
            mix = diff_attention(h, k, v, dist_bias, w_q[j], diff_lambda[j], g_subln[j],
                                 w_o[j], lambda_init)
        x = x + rms_norm(mix, g_post_mix[i])
        ffn = conv_ffn(rms_norm(x, g_pre_ffn[i]), w_ffn_up[i], w_ffn_conv[i], w_ffn_down[i])
        x = x + rms_norm(ffn, g_post_ffn[i])
        x = x + per_layer_embed(x, p[i], g_pre_ple[i], w_ple_gate[i], w_ple_proj[i], g_post_ple[i])
    return x
```

```python
import math
from contextlib import ExitStack

import numpy as np
import concourse.bass as bass
import concourse.mybir as mybir
from concourse.bass_utils import run_bass_kernel_spmd

F32 = mybir.dt.float32
BF16 = mybir.dt.bfloat16
AF = mybir.ActivationFunctionType
ALU = mybir.AluOpType
AX = mybir.AxisListType

D = 1024
S = 4096
T = 512
NT = S // T
KC = 8
DFF = 2816
FC = 22
NH = 8
EPS = 1e-6
LI = 0.8 - 0.6 * math.exp(-0.3 * 1)
NEG = -30000.0
SLOT = 4096
NSLOT = 6
NTMP = 10
NE = 4
N_TILES = NT


class Sem:
    def __init__(self, h):
        self.h = h
        self.count = 0


class TL:
    def __init__(self, name, const=False):
        self.name = name
        self.w = {}
        self.r = {}
        self.const = const


class BK:
    def __init__(self, ap, t):
        self.ap = ap
        self.t = t


ENG = ("pe", "act", "dve", "pool", "sp")


class Prog:
    def __init__(self, nc, es):
        self.nc = nc
        self.es = es
        self.q = {k: [] for k in ENG}
        self.seen = {k: {} for k in ENG}
        self.esem = {k: self.newsem("e_" + k) for k in ("pe", "act", "dve", "pool")}
        self.phase = "init"
        self.phases = {k: [] for k in ENG}

    def newsem(self, name):
        return Sem(self.es.enter_context(self.nc.semaphore(name)))

    def _deps(self, eng, reads, writes):
        waits = {}
        seen = self.seen[eng]
        pes = self.esem["pe"] if eng == "pe" else None

        def need(sem, val):
            if sem is pes:
                return
            if seen.get(sem, 0) < val and waits.get(sem, 0) < val:
                waits[sem] = val

        for t in reads:
            for sem, val in t.w.items():
                need(sem, val)
        for t in writes:
            for sem, val in t.w.items():
                need(sem, val)
            for sem, val in t.r.items():
                need(sem, val)
        return waits

    def _commit(self, eng, waits, reads, writes, tok):
        seen = self.seen[eng]
        for sem, val in waits.items():
            seen[sem] = val
        sem, val = tok
        for t in reads:
            if not t.const:
                if t.r.get(sem, 0) < val:
                    t.r[sem] = val
        for t in writes:
            if t.w.get(sem, 0) < val:
                t.w[sem] = val
            t.r = {}

    def op(self, eng, fn, reads=(), writes=(), signal=True):
        waits = self._deps(eng, reads, writes)
        es_ = self.esem[eng]
        if signal:
            es_.count += 1
            tok = (es_, es_.count)
        else:
            tok = (es_, es_.count + 1)
        self._commit(eng, waits, reads, writes, tok)
        self.q[eng].append((list(waits.items()), fn, tok if signal else None, 1))
        self.phases[eng].append(self.phase)

    def dma(self, eng, out, in_, reads, writes, sem):
        waits = self._deps(eng, reads, writes)
        if sem.count > 0 and self.seen[eng].get(sem, 0) < sem.count:
            waits[sem] = max(waits.get(sem, 0), sem.count)
        sem.count += 16
        tok = (sem, sem.count)
        self._commit(eng, waits, reads, writes, tok)
        self.q[eng].append((list(waits.items()), (lambda e: e.dma_start(out=out, in_=in_)), tok, 16))

    def final_wait(self, eng, sems):
        waits = [(s, s.count) for s in sems if s.count > 0]
        self.q[eng].append((waits, None, None, 0))

    def replay(self, name, e):
        for waits, fn, tok, inc in self.q[name]:
            for s, v in waits:
                e.wait_ge(s.h, v)
            if fn is None:
                continue
            ins = fn(e)
            if tok is not None:
                ins.then_inc(tok[0].h, inc)


def build_program(n_tiles=N_TILES, stage=6):
    nc = bass.Bass("TRN2", target_bir_lowering=False)

    def din(name, shape, dt=F32):
        return nc.dram_tensor(name, list(shape), dt, kind="ExternalInput").ap()

    xT = din("xT", [D, S])
    pT = din("pT", [2, 256, S])
    gv_d = din("gv", [128, 13 * 8])
    scw_d = din("scw", [128, 24])
    fcw_d = din("fcw", [128, 2 * 3 * 44])
    gsub_d = din("gsub", [128, 1])
    cb_d = din("cb", [128, NH])
    dl_d = din("dl", [128, 256])
    biasT = din("biasT", [NH, 128, 5 * T])
    ident_d = din("ident", [128, 128])
    w_sc_in = din("w_sc_in", [D, 3 * D])
    w_sc_out = din("w_sc_out", [D, D])
    w_kv = din("w_kv", [D, 2 * D])
    w_q = din("w_q", [D, D])
    w_o = din("w_o", [D, D])
    w_up = din("w_up", [2, D, 2 * DFF])
    w_down = din("w_down", [2, DFF, D])
    w_pg = din("w_pg", [2, D, D])
    w_pp = din("w_pp", [2, 256, D])
    outT = nc.dram_tensor("outT", [D, S], F32, kind="ExternalOutput").ap()

    panels = []
    pidx = {}

    def addp(key, src, k0, kcp, n0, pw):
        pidx[key] = len(panels)
        panels.append(dict(src=src, k0=k0, kcp=kcp, n0=n0, pw=pw))

    def add_ffn(l):
        seen_p = []
        for j in range(FC):
            for pn in (j // 4, (FC + j) // 4):
                if pn not in seen_p:
                    seen_p.append(pn)
                    addp(("up", l, pn), w_up[l], 0, 8, pn * 512, 512)
        for nh in range(2):
            for kg, (k0, kcp) in enumerate(((0, 8), (8, 8), (16, 6))):
                addp(("down", l, nh, kg), w_down[l], k0, kcp, nh * 512, 512)

    def add_ple(l):
        for pn in range(2):
            addp(("pg", l, pn), w_pg[l], 0, 8, pn * 512, 512)
        addp(("pp", l), w_pp[l], 0, 2, 0, 1024)

    for pn in (0, 2, 4, 1, 3, 5):
        addp(("scin", pn), w_sc_in, 0, 8, pn * 512, 512)
    for pn in range(2):
        addp(("scout", pn), w_sc_out, 0, 8, pn * 512, 512)
    add_ffn(0)
    add_ple(0)
    for pn in range(4):
        addp(("kv", pn), w_kv, 0, 8, pn * 512, 512)
    for pn in range(2):
        addp(("q", pn), w_q, 0, 8, pn * 512, 512)
    for pn in range(2):
        addp(("o", pn), w_o, 0, 8, pn * 512, 512)
    add_ffn(1)
    add_ple(1)
    NP = len(panels)

    wpan = nc.dram_tensor("wpan", [NP, 128, SLOT], BF16, kind="Internal").ap()
    KTd = nc.dram_tensor("KTd", [NH, 128, S], BF16, kind="Internal").ap()
    Vd = nc.dram_tensor("Vd", [NH, 128, S], BF16, kind="Internal").ap()
    expBd = nc.dram_tensor("expBd", [NH, 128, 5 * T], BF16, kind="Internal").ap()

    es = ExitStack()
    with es:
        P = Prog(nc, es)

        def sb(name, shape, dt):
            return es.enter_context(nc.sbuf_tensor(name, list(shape), dt))

        xbuf = [sb("x%d" % i, [128, KC, T], F32) for i in range(2)]
        x_t = [[TL("x%d_%d" % (i, c)) for c in range(KC)] for i in range(2)]
        ybuf = sb("ybuf", [128, KC, T], F32)
        y_t = [TL("y%d" % c) for c in range(KC)]
        hb = sb("hb", [128, KC, T], BF16)
        hb_t = TL("hb")
        hb2 = sb("hb2", [128, KC, T], BF16)
        hb2_t = TL("hb2")
        sq = sb("sq", [128, KC, T], BF16)
        sq_t = [TL("sq%d" % c) for c in range(KC)]
        actb = sb("actb", [128, FC * T], BF16)
        actb3 = actb[:].rearrange("p (c t) -> p c t", c=FC)
        act_t = [TL("act%d" % c) for c in range(FC)]
        vst = actb[:, 0:4 * 1024].rearrange("p (s f) -> p s f", s=4)
        vst_t = act_t[0:8]
        qz = [sb("qz%d" % m, [128, NH, T], BF16) for m in range(2)]
        qt_t = TL("qt")
        ring = sb("ring", [128, NSLOT * SLOT], BF16)
        slot_t = [TL("slot%d" % i) for i in range(NSLOT)]
        slot_sem = [P.newsem("slot%d" % i) for i in range(NSLOT)]
        tmps = [BK(sb("tmp%d" % i, [128, T + 2], F32)[:], TL("tmp%d" % i)) for i in range(NTMP)]
        ets = [BK(sb("et%d" % i, [128, 2, T], BF16), TL("et%d" % i)) for i in range(NE)]
        for e_ in ets:
            e_.tm = [TL(e_.t.name + "m0"), TL(e_.t.name + "m1")]
        pst = sb("pst", [128, 2, T], F32)
        pst_t = TL("pst")
        pb = sb("pb", [128, 2, T], BF16)
        pb_t = TL("pb")
        ones = sb("ones", [128, 128], BF16)
        ones32 = sb("ones32", [128, 128], F32)
        warm = sb("warm", [128, 2], F32)
        warm_t = TL("warm")
        ident32 = sb("ident32", [128, 128], F32)
        ident = sb("identb", [128, 128], BF16)
        accD = [BK(sb("acc%d" % a_, [128, 2, T], F32), TL("acc%d" % a_)) for a_ in range(2)]
        sqos = [BK(sb("sqo%d" % a_, [128, T], BF16)[:], TL("sqo%d" % a_)) for a_ in range(2)]
        ones_t = TL("ones", const=True)
        gv = sb("gvs", [128, 13 * 8], F32)
        scw = sb("scws", [128, 24], F32)
        fcw = sb("fcws", [128, 2 * 3 * 44], F32)
        gsub = sb("gsubs", [128, 1], F32)
        cb = sb("cbs", [128, NH], F32)
        ncb = sb("ncbs", [128, NH], F32)
        dl = sb("dls", [128, 256], F32)
        lt = sb("lt", [128, 128], F32)
        ls = sb("lss", [128, 4], F32)
        nlam = sb("nlam", [128, 1], F32)
        cst_t = TL("consts", const=True)
        halo_sc = sb("halo_sc", [128, KC, 2], F32)
        halo_sc_t = TL("halo_sc")
        halo_f = sb("halo_f", [128, 2 * 44, 2], F32)
        halo_f_t = TL("halo_f")
        banks = [BK(es.enter_context(nc.psum_tensor("ps%d" % i, [128, T], F32))[:], TL("ps%d" % i)) for i in range(8)]

        rr = {"s3": 0, "bank": 0, "sbank": 0, "tmp": 0, "et": 0, "slot": 0, "cv": 0, "io": 0}

        pinned = set()

        def bank():
            while True:
                b = banks[rr["bank"] % 8]
                rr["bank"] += 1
                if b.t.name not in pinned:
                    return b

        stat = {"bank": None, "queue": [], "n": 0}

        def stat_begin():
            b = bank()
            pinned.add(b.t.name)
            stat["bank"] = b
            stat["queue"] = []
            stat["n"] = 0

        def _stat_mm(c, last):
            b = stat["bank"]
            mm(b.ap, ones[:], sq[:, c, :], stat["n"] == 0, last, [ones_t, sq_t[c]], [b.t], signal=last)
            stat["n"] += 1

        def stat_push(c):
            if stat["bank"] is not None:
                stat["queue"].append(c)

        def stat_flush():
            if stat["bank"] is None:
                return
            for c in stat["queue"]:
                _stat_mm(c, False)
            stat["queue"] = []

        def stat_finish():
            q_ = stat["queue"]
            for k_, c in enumerate(q_):
                _stat_mm(c, k_ == len(q_) - 1)
            b = stat["bank"]
            assert stat["n"] == KC and q_, "stat accumulation incomplete"
            pinned.discard(b.t.name)
            stat["bank"] = None
            stat["queue"] = []
            rs = tmp()
            act(rs.ap[:, 0:T], b.ap, AF.Ln, [b.t], [rs.t], bias=EPS, scale=1.0 / D)
            act(rs.ap[:, 0:T], rs.ap[:, 0:T], AF.Exp, [rs.t], [rs.t], scale=-0.5)
            return rs

        def sbank():
            b = banks[rr["sbank"] % 4]
            rr["sbank"] += 1
            return b

        def tmp():
            b = tmps[rr["tmp"] % NTMP]
            rr["tmp"] += 1
            return b

        def etile():
            b = ets[rr["et"] % NE]
            rr["et"] += 1
            return b

        cv_sems = [P.newsem("cv%d" % i) for i in range(8)]
        io_sems = [P.newsem("io%d" % i) for i in range(8)]
        out_sems = [P.newsem("out%d" % i) for i in range(2)]

        pio_sems = [P.newsem("pio%d" % i) for i in range(8)]
        rr["pio"] = 0

        def io_sem():
            s_ = io_sems[rr["io"] % 8]
            rr["io"] += 1
            return s_

        def pio_sem():
            s_ = pio_sems[rr["pio"] % 8]
            rr["pio"] += 1
            return s_

        def mm(out, lhsT, rhs, start, stop, reads, writes, signal=False):
            P.op("pe", lambda e: e.matmul(out, lhsT=lhsT, rhs=rhs, start=start, stop=stop), reads, writes, signal)

        def act(out, in_, func, reads, writes, bias=None, scale=None):
            kw = {}
            if bias is not None:
                kw["bias"] = bias
            if scale is not None:
                kw["scale"] = scale
            P.op("act", lambda e: e.activation(out=out, in_=in_, func=func, **kw), reads, writes)

        def tt(eng, out, in0, in1, op, reads, writes):
            P.op(eng, lambda e: e.tensor_tensor(out=out, in0=in0, in1=in1, op=op), reads, writes)

        def stt(eng, out, in0, scalar, in1, op0, op1, reads, writes):
            P.op(eng, lambda e: e.scalar_tensor_tensor(out=out, in0=in0, scalar=scalar, in1=in1, op0=op0, op1=op1),
                 reads, writes)

        def cp(eng, out, in_, reads, writes):
            P.op(eng, lambda e: e.tensor_copy(out=out, in_=in_), reads, writes)

        pan_t = [TL("pan%d" % i) for i in range(NP)]
        conv_done = [0]

        def emit_conv(upto):
            while conv_done[0] < min(upto, NP):
                i = conv_done[0]
                pd = panels[i]
                kcp, pw = pd["kcp"], pd["pw"]
                src = pd["src"][pd["k0"] * 128:(pd["k0"] + kcp) * 128, pd["n0"]:pd["n0"] + pw]
                src = src.rearrange("(kc p) n -> p kc n", p=128)
                dst = wpan[i][:, 0:kcp * pw].rearrange("p (kc n) -> p kc n", kc=kcp)
                s_ = cv_sems[rr["cv"] % 8]
                rr["cv"] += 1
                P.dma("pool", dst, src, [], [pan_t[i]], s_)
                conv_done[0] += 1

        slot_gen = [0] * NSLOT

        def load_slot(src_ap, n, src_tiles):
            k = rr["slot"] % NSLOT
            rr["slot"] += 1
            dst = ring[:, k * SLOT:k * SLOT + n]
            P.dma("sp", dst, src_ap, src_tiles, [slot_t[k]], slot_sem[k])
            b = BK(dst, slot_t[k])
            b.k = k
            b.gen = rr["slot"]
            slot_gen[k] = b.gen
            return b

        def load_panel(key):
            i = pidx[key]
            emit_conv(i + 10)
            pd = panels[i]
            n = pd["kcp"] * pd["pw"]
            b = load_slot(wpan[i][:, 0:n], n, [pan_t[i]])
            b2 = BK(b.ap.rearrange("p (kc n) -> p kc n", kc=pd["kcp"]), b.t)
            b2.k = b.k
            b2.gen = b.gen
            return b2

        s0 = io_sem()
        for dst, src in ((gv, gv_d), (scw, scw_d), (fcw, fcw_d), (gsub, gsub_d), (cb, cb_d), (dl, dl_d)):
            P.dma("sp", dst[:], src, [], [cst_t], s0)
        P.dma("sp", ident32[:], ident_d, [], [ones_t], s0)
        P.op("dve", lambda e: e.memset(ones[:], 1.0), [], [ones_t])
        P.op("dve", lambda e: e.memset(warm[:], 1.0), [], [warm_t])
        P.op("dve", lambda e: e.tensor_copy(out=ident[:], in_=ident32[:]), [ones_t], [ones_t])
        P.op("dve", lambda e: e.memset(ones32[:], 1.0), [], [ones_t])
        P.op("dve", lambda e: e.memset(qz[0][:], 0.0), [], [qt_t])
        P.op("dve", lambda e: e.memset(qz[1][:], 0.0), [], [qt_t])
        P.op("dve", lambda e: e.memset(halo_sc[:], 0.0), [], [halo_sc_t])
        P.op("dve", lambda e: e.memset(halo_f[:], 0.0), [], [halo_f_t])
        tt("dve", lt[:, 0:64], dl[:, 0:64], dl[:, 64:128], ALU.mult, [cst_t], [cst_t])
        tt("dve", lt[:, 64:128], dl[:, 128:192], dl[:, 192:256], ALU.mult, [cst_t], [cst_t])
        P.op("dve", lambda e: e.reduce_sum(out=ls[:, 0:1], in_=lt[:, 0:64], axis=AX.X), [cst_t], [cst_t])
        P.op("dve", lambda e: e.reduce_sum(out=ls[:, 1:2], in_=lt[:, 64:128], axis=AX.X), [cst_t], [cst_t])
        act(ls[:, 2:4], ls[:, 0:2], AF.Exp, [cst_t], [cst_t])
        tt("dve", nlam[:, 0:1], ls[:, 3:4], ls[:, 2:3], ALU.subtract, [cst_t], [cst_t])
        P.op("dve", lambda e: e.tensor_scalar_add(out=nlam[:, 0:1], in0=nlam[:, 0:1], scalar1=-LI), [cst_t], [cst_t])
        P.op("dve", lambda e: e.tensor_scalar_mul(out=gsub[:, 0:1], in0=gsub[:, 0:1], scalar1=(1.0 - LI)), [cst_t], [cst_t])
        P.op("dve", lambda e: e.tensor_scalar_mul(out=ncb[:], in0=cb[:], scalar1=-1.0), [cst_t], [cst_t])
        expB_t = [TL("expB%d" % h) for h in range(NH)]
        P.dma("sp", xbuf[0][:], xT.rearrange("(c p) t -> p c t", p=128)[:, :, 0:T], [], x_t[0], io_sem())
        emit_conv(12)
        for h in range(NH):
            P.dma("pool", expBd[h].rearrange("p (o q) -> p o q", o=5), biasT[h].rearrange("p (o q) -> p o q", o=5),
                  [], [expB_t[h]], pio_sem())

        emit_conv(12)

        KT_dt = TL("KTd")
        V_dt = TL("Vd")

        def sumsq_rstd(nchunks, dim):
            b = bank()
            for c in range(nchunks):
                mm(b.ap, ones[:], sq[:, c, :], c == 0, c == nchunks - 1, [ones_t, sq_t[c]], [b.t], signal=(c == nchunks - 1))
            rs = tmp()
            act(rs.ap[:, 0:T], b.ap, AF.Ln, [b.t], [rs.t], bias=EPS, scale=1.0 / dim)
            act(rs.ap[:, 0:T], rs.ap[:, 0:T], AF.Exp, [rs.t], [rs.t], scale=-0.5)
            return rs

        def prenorm(xi, outs, skip_square=False):
            x = xbuf[xi]
            if not skip_square:
                for half in range(2):
                    c0, c1 = 4 * half, 4 * half + 4
                    act(sq[:, c0:c1, :], x[:, c0:c1, :], AF.Square, x_t[xi][c0:c1], sq_t[c0:c1])
            rs = sumsq_rstd(KC, D)
            for gidx, ob, ot, eng in outs:
                for c in range(KC):
                    stt(eng, ob[:, c, :], x[:, c, :], gv[:, gidx * 8 + c:gidx * 8 + c + 1], rs.ap[:, 0:T],
                        ALU.mult, ALU.mult, [x_t[xi][c], rs.t, cst_t], [ot])

        def postnorm_res(xi, gidx, sq_next=False):
            x = xbuf[xi]
            rs = stat_finish() if stat["bank"] is not None else sumsq_rstd(KC, D)
            for c in range(KC):
                tt("dve", ybuf[:, c, :], ybuf[:, c, :], rs.ap[:, 0:T], ALU.mult, [y_t[c], rs.t], [y_t[c]])
                stt("dve", x[:, c, :], ybuf[:, c, :], gv[:, gidx * 8 + c:gidx * 8 + c + 1], x[:, c, :],
                    ALU.mult, ALU.add, [y_t[c], x_t[xi][c], cst_t], [x_t[xi][c]])
                if sq_next:
                    act(sq[:, c, :], x[:, c, :], AF.Square, [x_t[xi][c]], [sq_t[c]])

        def evac_y(oc, b):
            act(sq[:, oc, :], b.ap, AF.Square, [], [sq_t[oc], b.t])
            cp("dve", ybuf[:, oc, :], b.ap, [b.t], [y_t[oc]])
            stat_push(oc)

        def proj8(pkeys, inb, in_t, evac):
            pn = None
            for oc in range(8):
                if oc % 4 == 0:
                    pn = load_panel(pkeys[oc // 4])
                b = bank()
                for kc in range(KC):
                    mm(b.ap, pn.ap[:, kc, (oc % 4) * 128:(oc % 4 + 1) * 128], inb[:, kc, :], kc == 0, kc == KC - 1,
                       [pn.t, in_t], [b.t], signal=(kc == KC - 1))
                stat_flush()
                evac(oc, b)

        def conv_taps(src, acc, wt, wbase, wstride, reads_src):
            act(acc.ap[:, 0:T], src.ap[:, 2:T + 2], AF.Copy, [src.t, cst_t], [acc.t],
                scale=wt[:, wbase + 2 * wstride:wbase + 2 * wstride + 1])
            stt("dve", acc.ap[:, 0:T], src.ap[:, 1:T + 1], wt[:, wbase + wstride:wbase + wstride + 1], acc.ap[:, 0:T],
                ALU.mult, ALU.add, [src.t, acc.t, cst_t], [acc.t])
            stt("dve", acc.ap[:, 0:T], src.ap[:, 0:T], wt[:, wbase:wbase + 1], acc.ap[:, 0:T],
                ALU.mult, ALU.add, [src.t, acc.t, cst_t], [acc.t])

        def sc_mixer(xi):
            P.phase = "scmix"
            prenorm(xi, [(0, hb, hb_t, "dve")])
            pn = {}
            for c in range(KC):
                if c % 4 == 0:
                    for g_ in range(3):
                        pn[g_] = load_panel(("scin", 2 * g_ + c // 4))
                bs = []
                for g_ in range(3):
                    b = bank()
                    for kc in range(KC):
                        mm(b.ap, pn[g_].ap[:, kc, (c % 4) * 128:(c % 4 + 1) * 128], hb[:, kc, :], kc == 0, kc == KC - 1,
                           [pn[g_].t, hb_t], [b.t], signal=(kc == KC - 1))
                    bs.append(b)
                gcs = tmp()
                act(gcs.ap[:, 0:T], bs[1].ap, AF.Copy, [bs[1].t], [gcs.t])
                pr = tmp()
                cp("pool", pr.ap[:, 0:2], halo_sc[:, c, :], [halo_sc_t], [pr.t])
                tt("dve", pr.ap[:, 2:T + 2], gcs.ap[:, 0:T], bs[2].ap, ALU.mult, [gcs.t, bs[2].t], [pr.t])
                cp("pool", halo_sc[:, c, :], pr.ap[:, T:T + 2], [pr.t], [halo_sc_t])
                acc = tmp()
                conv_taps(pr, acc, scw, c, 8, None)
                tt("dve", hb2[:, c, :], acc.ap[:, 0:T], bs[0].ap, ALU.mult, [acc.t, bs[0].t], [hb2_t])
            stat_begin()
            proj8([("scout", 0), ("scout", 1)], hb2, hb2_t, evac_y)
            postnorm_res(xi, 1, sq_next=True)

        def ffn(xi, l):
            P.phase = "ffn_up"
            prenorm(xi, [(6 * l + 2, hb, hb_t, "dve")], skip_square=True)
            pcur = {}
            for j in range(FC):
                accs = []
                for which, cidx in ((0, j), (1, FC + j)):
                    pnum = cidx // 4
                    if pnum not in pcur or slot_gen[pcur[pnum].k] != pcur[pnum].gen:
                        pcur[pnum] = load_panel(("up", l, pnum))
                    pn = pcur[pnum]
                    b = bank()
                    for kc in range(KC):
                        mm(b.ap, pn.ap[:, kc, (cidx % 4) * 128:(cidx % 4 + 1) * 128], hb[:, kc, :], kc == 0, kc == KC - 1,
                           [pn.t, hb_t], [b.t], signal=(kc == KC - 1))
                    yc = tmp()
                    hidx = l * 44 + cidx
                    cp("pool", yc.ap[:, 0:2], halo_f[:, hidx, :], [halo_f_t], [yc.t])
                    act(yc.ap[:, 2:T + 2], b.ap, AF.Copy, [b.t], [yc.t])
                    cp("pool", halo_f[:, hidx, :], yc.ap[:, T:T + 2], [yc.t], [halo_f_t])
                    acc = tmp()
                    conv_taps(yc, acc, fcw, l * 132 + cidx, 44, None)
                    accs.append(acc)
                sg = tmp()
                act(sg.ap[:, 0:T], accs[0].ap[:, 0:T], AF.Silu, [accs[0].t], [sg.t])
                tt("dve", actb3[:, j, :], sg.ap[:, 0:T], accs[1].ap[:, 0:T], ALU.mult, [sg.t, accs[1].t], [act_t[j]])
                if j == FC - 1:
                    act(warm[:, 0:1], ones32[:, 0:1], AF.Ln, [ones_t], [warm_t])
            P.phase = "ffn_down"
            stat_begin()
            for nh in range(2):
                bs = [bank() for _ in range(4)]
                for kg, (k0, kcp) in enumerate(((0, 8), (8, 8), (16, 6))):
                    pn = load_panel(("down", l, nh, kg))
                    for ocl in range(4):
                        for kcl in range(kcp):
                            first = (kg == 0 and kcl == 0)
                            last = (kg == 2 and kcl == kcp - 1)
                            mm(bs[ocl].ap, pn.ap[:, kcl, ocl * 128:(ocl + 1) * 128], actb3[:, k0 + kcl, :], first, last,
                               [pn.t, act_t[k0 + kcl]], [bs[ocl].t], signal=(kcl == kcp - 1))
                        stat_flush()
                for ocl in range(4):
                    evac_y(nh * 4 + ocl, bs[ocl])
            postnorm_res(xi, 6 * l + 3, sq_next=True)

        def ple(xi, l, t0):
            P.phase = "ple"
            prenorm(xi, [(6 * l + 4, hb, hb_t, "dve")], skip_square=True)
            P.dma("sp", pst[:], pT[l].rearrange("(c p) t -> p c t", p=128)[:, :, t0:t0 + T], [], [pst_t], io_sem())
            act(pb[:], pst[:], AF.Copy, [pst_t], [pb_t])
            pp = load_panel(("pp", l))
            pg = None
            stat_begin()
            for oc in range(8):
                if oc % 4 == 0:
                    pg = load_panel(("pg", l, oc // 4))
                ba = bank()
                for kc in range(KC):
                    mm(ba.ap, pg.ap[:, kc, (oc % 4) * 128:(oc % 4 + 1) * 128], hb[:, kc, :], kc == 0, kc == KC - 1,
                       [pg.t, hb_t], [ba.t], signal=(kc == KC - 1))
                bb = bank()
                for kc in range(2):
                    mm(bb.ap, pp.ap[:, kc, oc * 128:(oc + 1) * 128], pb[:, kc, :], kc == 0, kc == 1,
                       [pp.t, pb_t], [bb.t], signal=(kc == 1))
                stat_flush()
                sg = tmp()
                act(sg.ap[:, 0:T], ba.ap, AF.Sigmoid, [ba.t], [sg.t])
                tt("dve", ybuf[:, oc, :], sg.ap[:, 0:T], bb.ap, ALU.mult, [sg.t, bb.t], [y_t[oc]])
                act(sq[:, oc, :], ybuf[:, oc, :], AF.Square, [y_t[oc]], [sq_t[oc]])
                stat_push(oc)
                if oc == 7:
                    act(warm[:, 0:1], ones32[:, 0:1], AF.Ln, [ones_t], [warm_t])
            postnorm_res(xi, 6 * l + 5, sq_next=(l == 0))

        def attention(xi, i, t0):
            P.phase = "kvq"
            prenorm(xi, [(12, hb2, hb2_t, "dve"), (6, hb, hb_t, "dve")], skip_square=True)
            kst = sq

            def evac_k(oc, b):
                act(kst[:, oc, :], b.ap, AF.Copy, [b.t], [sq_t[oc]])

            proj8([("kv", 0), ("kv", 1)], hb2, hb2_t, evac_k)
            P.dma("pool", KTd.rearrange("h p t -> p h t")[:, :, t0:t0 + T], kst[:], sq_t, [KT_dt], pio_sem())
            for nh in range(2):
                pn = load_panel(("kv", 2 + nh))
                for s_ in range(4):
                    b = bank()
                    for kc in range(KC):
                        mm(b.ap, hb2[:, kc, s_ * 128:(s_ + 1) * 128], pn.ap[:, kc, :], kc == 0, kc == KC - 1,
                           [pn.t, hb2_t], [b.t], signal=(kc == KC - 1))
                    if s_ % 2 == 0:
                        cp("dve", vst[:, s_, nh * 512:(nh + 1) * 512], b.ap, [b.t], vst_t)
                    else:
                        act(vst[:, s_, nh * 512:(nh + 1) * 512], b.ap, AF.Copy, [b.t], vst_t)
            for h in range(NH):
                dstv = Vd[h].rearrange("p (b e) -> p b e", e=128)[:, 4 * i:4 * i + 4, :]
                P.dma("pool", dstv, vst[:, :, h * 128:(h + 1) * 128], vst_t, [V_dt], pio_sem())

            def evac_q(oc, b):
                act(qz[0][0:64, oc, :], b.ap[0:64, :], AF.Copy, [b.t], [qt_t], scale=0.125)
                act(qz[1][64:128, oc, :], b.ap[64:128, :], AF.Copy, [b.t], [qt_t], scale=0.125)

            proj8([("q", 0), ("q", 1)], hb, hb_t, evac_q)

            nk = 4 * (i + 1)
            ao = hb2
            pending = []

            def run_pending(jnow):
                keep = []
                for trig, fn_ in pending:
                    if trig <= jnow:
                        fn_()
                    else:
                        keep.append((trig, fn_))
                pending[:] = keep

            for hh in range(NH):
                P.phase = "attn"
                ktp = load_slot(KTd[hh][:, 0:nk * 128], nk * 128, [KT_dt])
                vpb = load_slot(Vd[hh][:, 0:nk * 128], nk * 128, [V_dt])
                vp = vpb.ap.rearrange("p (b e) -> p b e", e=128)
                ebb = load_slot(expBd[hh], 5 * T, [expB_t[hh]])
                eb = ebb.ap.rearrange("p (o q) -> p o q", o=5)
                par = hh % 2
                O = banks[4 + 2 * par:6 + 2 * par]
                acc = accD[par]

                D0 = banks[3]

                def emit_qk_m(j, m, ktp=ktp, hh=hh, ebb=ebb, eb=eb):
                    o_ = T * i - 128 * j
                    special = o_ <= 128
                    qs = max(0, -o_)
                    sb_ = banks[rr["s3"] % 3]
                    rr["s3"] += 1
                    mm(sb_.ap[:, qs:T], ktp.ap[:, j * 128:(j + 1) * 128], qz[m][:, hh, qs:T],
                       True, not special, [ktp.t, qt_t], [sb_.t], signal=(not special))
                    if special:
                        oi = (o_ + 384) // 128
                        mm(sb_.ap[:, qs:T], ident[:], eb[:, oi, qs:T], False, True, [ones_t, ebb.t], [sb_.t], signal=True)
                    return sb_

                Sb = [emit_qk_m(0, 0), emit_qk_m(0, 1)]
                for j in range(nk):
                    o_ = T * i - 128 * j
                    qs = max(0, -o_)
                    E = etile()
                    Sn = [None, None]
                    for m in range(2):
                        if j + 1 < nk:
                            Sn[m] = emit_qk_m(j + 1, m)
                        if o_ <= 128:
                            act(E.ap[:, m, qs:T], Sb[m].ap[:, qs:T], AF.Exp, [Sb[m].t], [E.tm[m]])
                        else:
                            act(E.ap[:, m, :], Sb[m].ap, AF.Exp, [Sb[m].t, cst_t], [E.tm[m]], bias=cb[:, hh:hh + 1])
                    for m in range(2):
                        mm(O[m].ap[:, qs:T], vp[:, j, :], E.ap[:, m, qs:T], j == 0, j == nk - 1, [vpb.t, E.tm[m]], [O[m].t],
                           signal=True)
                    mm(D0.ap[:, qs:T], ones[:], E.ap[:, 0, qs:T], j == 0, j == nk - 1, [ones_t, E.tm[0]], [D0.t], signal=True)
                    if j == 0:
                        cp("dve", acc.ap[:, 1, :], E.ap[:, 1, :], [E.tm[1]], [acc.t])
                    else:
                        tt("dve", acc.ap[:, 1, qs:T], acc.ap[:, 1, qs:T], E.ap[:, 1, qs:T], ALU.add, [acc.t, E.tm[1]], [acc.t])
                    Sb = Sn
                    run_pending(j)
                P.phase = "attn_fin"
                r0 = tmp()
                r1 = tmp()
                rs = tmp()
                sqo = sqos[par]
                act(r0.ap[:, 0:T], D0.ap, AF.Ln, [D0.t], [r0.t])
                act(r0.ap[:, 0:T], r0.ap[:, 0:T], AF.Exp, [r0.t], [r0.t], scale=-1.0)

                def fin_a1(r0=r0, O=O):
                    tt("dve", r0.ap[:, 0:T], O[0].ap, r0.ap[:, 0:T], ALU.mult, [O[0].t, r0.t], [r0.t])

                def fin_a2(O=O, acc=acc):
                    mm(O[0].ap, ones32[:], acc.ap[:, 1, :], True, True, [ones_t, acc.t], [O[0].t], signal=True)

                def fin_a3(r1=r1, O=O):
                    act(r1.ap[:, 0:T], O[0].ap, AF.Ln, [O[0].t], [r1.t])
                    act(r1.ap[:, 0:T], r1.ap[:, 0:T], AF.Exp, [r1.t], [r1.t], scale=-1.0)

                def fin_a(r0=r0, r1=r1, O=O):
                    tt("dve", r1.ap[:, 0:T], O[1].ap, r1.ap[:, 0:T], ALU.mult, [O[1].t, r1.t], [r1.t])
                    stt("dve", r0.ap[:, 0:T], r1.ap[:, 0:T], nlam[:, 0:1], r0.ap[:, 0:T], ALU.mult, ALU.add,
                        [r0.t, r1.t, cst_t], [r0.t])

                def fin_b(r0=r0, sqo=sqo, O=O):
                    act(sqo.ap, r0.ap[:, 0:T], AF.Square, [r0.t], [sqo.t])
                    mm(O[0].ap, ones[:], sqo.ap, True, True, [ones_t, sqo.t], [O[0].t], signal=True)

                def fin_c(rs=rs, O=O):
                    act(rs.ap[:, 0:T], O[0].ap, AF.Ln, [O[0].t], [rs.t], bias=EPS, scale=1.0 / 128)
                    act(rs.ap[:, 0:T], rs.ap[:, 0:T], AF.Exp, [rs.t], [rs.t], scale=-0.5)

                def fin_d(r0=r0, rs=rs, hh=hh):
                    stt("dve", ao[:, hh, :], r0.ap[:, 0:T], gsub[:, 0:1], rs.ap[:, 0:T], ALU.mult, ALU.mult,
                        [r0.t, rs.t, cst_t], [hb2_t])

                pending.append((0, fin_a1))
                pending.append((min(1, nk - 1), fin_a2))
                pending.append((min(2, nk - 1), fin_a3))
                pending.append((min(3, nk - 1), fin_a))
                pending.append((min(4, nk - 1), fin_b))
                pending.append((min(5, nk - 1), fin_c))
                pending.append((min(6, nk - 1), fin_d))
            run_pending(10 ** 9)
            P.phase = "wo"
            stat_begin()
            proj8([("o", 0), ("o", 1)], ao, hb2_t, evac_y)
            postnorm_res(xi, 7, sq_next=True)

        for i in range(n_tiles):
            xi = i % 2
            t0 = i * T
            if i > 0:
                P.dma("sp", xbuf[xi][:], xT.rearrange("(c p) t -> p c t", p=128)[:, :, t0:t0 + T], [], x_t[xi], io_sem())
            sc_mixer(xi)
            if stage >= 2:
                ffn(xi, 0)
            if stage >= 3:
                ple(xi, 0, t0)
            if stage >= 4:
                attention(xi, i, t0)
            if stage >= 5:
                ffn(xi, 1)
            if stage >= 6:
                ple(xi, 1, t0)
            P.dma("pool", outT.rearrange("(c p) t -> p c t", p=128)[:, :, t0:t0 + T], xbuf[xi][:], x_t[xi], [], out_sems[xi])
        P.final_wait("pool", out_sems)

        with nc.Block() as block:
            @block.tensor
            def _(e):
                P.replay("pe", e)

            @block.scalar
            def _(e):
                P.replay("act", e)

            @block.vector
            def _(e):
                P.replay("dve", e)

            @block.gpsimd
            def _(e):
                P.replay("pool", e)

            @block.sync
            def _(e):
                P.replay("sp", e)
    build_program.last_prog = P
    return nc


def _rel_bucket_np(dist):
    max_exact = 16
    d = np.maximum(dist, 1).astype(np.float32)
    large = max_exact + (np.log(d / np.float32(max_exact)) / np.float32(math.log(128 / max_exact))
                         * np.float32(32 - max_exact)).astype(np.int32)
    large = np.minimum(large, 31)
    return np.where(dist < max_exact, dist, large)


def _host_inputs(inp, b):
    f = np.float32
    x = np.asarray(inp["x"], f)
    p = np.asarray(inp["p"], f)
    m = {}
    m["xT"] = np.ascontiguousarray(x[b].T)
    m["pT"] = np.ascontiguousarray(p[:, b].transpose(0, 2, 1))
    gl = []
    for l in range(2):
        for nm in ("g_pre_mix", "g_post_mix", "g_pre_ffn", "g_post_ffn", "g_pre_ple", "g_post_ple"):
            gl.append(np.asarray(inp[nm], f)[l])
    gl.append(np.asarray(inp["g_kv"], f))
    g = np.stack(gl, 0).reshape(13, 8, 128).transpose(2, 0, 1).reshape(128, 104)
    m["gv"] = np.ascontiguousarray(g)
    scw = np.asarray(inp["w_sc_conv"], f)[0].reshape(3, 8, 128).transpose(2, 0, 1).reshape(128, 24)
    m["scw"] = np.ascontiguousarray(scw)
    fcw = np.asarray(inp["w_ffn_conv"], f).reshape(2, 3, 44, 128).transpose(3, 0, 1, 2).reshape(128, 264)
    m["fcw"] = np.ascontiguousarray(fcw)
    m["gsub"] = np.ascontiguousarray(np.asarray(inp["g_subln"], f)[0].reshape(128, 1))
    rb = np.asarray(inp["rel_bias"], f)
    m["cb"] = np.ascontiguousarray(np.broadcast_to(rb[31:32, :], (128, NH)))
    m["dl"] = np.ascontiguousarray(np.broadcast_to(np.asarray(inp["diff_lambda"], f)[0].reshape(1, 256), (128, 256)))
    kk = np.arange(128)[:, None]
    qq = np.arange(T)[None, :]
    bt = np.empty((NH, 128, 5, T), f)
    for oi in range(5):
        o_ = -384 + 128 * oi
        d = o_ + qq - kk
        bk = _rel_bucket_np(np.maximum(d, 0))
        vals = rb[bk]
        vals = np.where((d >= 0)[:, :, None], vals, f(NEG))
        bt[:, :, oi, :] = vals.transpose(2, 0, 1)
    m["biasT"] = np.ascontiguousarray(bt.reshape(NH, 128, 5 * T))
    m["ident"] = np.eye(128, dtype=f)
    m["w_sc_in"] = np.ascontiguousarray(np.asarray(inp["w_sc_in"], f)[0])
    m["w_sc_out"] = np.ascontiguousarray(np.asarray(inp["w_sc_out"], f)[0])
    m["w_kv"] = np.ascontiguousarray(np.asarray(inp["w_kv"], f))
    m["w_q"] = np.ascontiguousarray(np.asarray(inp["w_q"], f)[0])
    m["w_o"] = np.ascontiguousarray(np.asarray(inp["w_o"], f)[0])
    m["w_up"] = np.ascontiguousarray(np.asarray(inp["w_ffn_up"], f))
    m["w_down"] = np.ascontiguousarray(np.asarray(inp["w_ffn_down"], f))
    m["w_pg"] = np.ascontiguousarray(np.asarray(inp["w_ple_gate"], f))
    m["w_pp"] = np.ascontiguousarray(np.asarray(inp["w_ple_proj"], f))
    return m


def kernel(**inputs):
    nc = build_program()
    shared = None
    in_maps = []
    for b in range(8):
        m = _host_inputs(inputs, b) if shared is None else None
        if shared is None:
            shared = m
        else:
            f = np.float32
            m = dict(shared)
            m["xT"] = np.ascontiguousarray(np.asarray(inputs["x"], f)[b].T)
            m["pT"] = np.ascontiguousarray(np.asarray(inputs["p"], f)[:, b].transpose(0, 2, 1))
        in_maps.append(m)
    res = run_bass_kernel_spmd(nc, in_maps, core_ids=list(range(8)))
    out = np.stack([np.asarray(r["outT"]).T for r in res.results], 0)
    return np.ascontiguousarray(out.astype(np.float32))
```

```python
import math
from contextlib import ExitStack

import numpy as np
import concourse.bass as bass
import concourse.mybir as mybir
from concourse.bass_utils import run_bass_kernel_spmd

F32 = mybir.dt.float32
BF16 = mybir.dt.bfloat16
AF = mybir.ActivationFunctionType
ALU = mybir.AluOpType
AX = mybir.AxisListType

D = 1024
S = 4096
T = 512
NT = S // T
KC = 8
DFF = 2816
FC = 22
NH = 8
EPS = 1e-6
LI = 0.8 - 0.6 * math.exp(-0.3 * 1)
NEG = -30000.0
SLOT = 4096
NSLOT = 6
NTMP = 10
NE = 4
N_TILES = NT


class Sem:
    def __init__(self, h):
        self.h = h
        self.count = 0


class TL:
    def __init__(self, name, const=False):
        self.name = name
        self.w = {}
        self.r = {}
        self.const = const


class BK:
    def __init__(self, ap, t):
        self.ap = ap
        self.t = t


ENG = ("pe", "act", "dve", "pool", "sp")


class Prog:
    def __init__(self, nc, es):
        self.nc = nc
        self.es = es
        self.q = {k: [] for k in ENG}
        self.seen = {k: {} for k in ENG}
        self.esem = {k: self.newsem("e_" + k) for k in ("pe", "act", "dve", "pool")}
        self.phase = "init"
        self.phases = {k: [] for k in ENG}

    def newsem(self, name):
        return Sem(self.es.enter_context(self.nc.semaphore(name)))

    def _deps(self, eng, reads, writes):
        waits = {}
        seen = self.seen[eng]
        pes = self.esem["pe"] if eng == "pe" else None

        def need(sem, val):
            if sem is pes:
                return
            if seen.get(sem, 0) < val and waits.get(sem, 0) < val:
                waits[sem] = val

        for t in reads:
            for sem, val in t.w.items():
                need(sem, val)
        for t in writes:
            for sem, val in t.w.items():
                need(sem, val)
            for sem, val in t.r.items():
                need(sem, val)
        return waits

    def _commit(self, eng, waits, reads, writes, tok):
        seen = self.seen[eng]
        for sem, val in waits.items():
            seen[sem] = val
        sem, val = tok
        for t in reads:
            if not t.const:
                if t.r.get(sem, 0) < val:
                    t.r[sem] = val
        for t in writes:
            if t.w.get(sem, 0) < val:
                t.w[sem] = val
            t.r = {}

    def op(self, eng, fn, reads=(), writes=(), signal=True):
        waits = self._deps(eng, reads, writes)
        es_ = self.esem[eng]
        if signal:
            es_.count += 1
            tok = (es_, es_.count)
        else:
            tok = (es_, es_.count + 1)
        self._commit(eng, waits, reads, writes, tok)
        self.q[eng].append((list(waits.items()), fn, tok if signal else None, 1))
        self.phases[eng].append(self.phase)

    def dma(self, eng, out, in_, reads, writes, sem):
        waits = self._deps(eng, reads, writes)
        if sem.count > 0 and self.seen[eng].get(sem, 0) < sem.count:
            waits[sem] = max(waits.get(sem, 0), sem.count)
        sem.count += 16
        tok = (sem, sem.count)
        self._commit(eng, waits, reads, writes, tok)
        self.q[eng].append((list(waits.items()), (lambda e: e.dma_start(out=out, in_=in_)), tok, 16))

    def final_wait(self, eng, sems):
        waits = [(s, s.count) for s in sems if s.count > 0]
        self.q[eng].append((waits, None, None, 0))

    def replay(self, name, e):
        for waits, fn, tok, inc in self.q[name]:
            for s, v in waits:
                e.wait_ge(s.h, v)
            if fn is None:
                continue
            ins = fn(e)
            if tok is not None:
                ins.then_inc(tok[0].h, inc)


def build_program(n_tiles=N_TILES, stage=6):
    nc = bass.Bass("TRN2", target_bir_lowering=False)

    def din(name, shape, dt=F32):
        return nc.dram_tensor(name, list(shape), dt, kind="ExternalInput").ap()

    xT = din("xT", [D, S])
    pT = din("pT", [2, 256, S])
    gv_d = din("gv", [128, 13 * 8])
    scw_d = din("scw", [128, 24])
    fcw_d = din("fcw", [128, 2 * 3 * 44])
    gsub_d = din("gsub", [128, 1])
    cb_d = din("cb", [128, NH])
    dl_d = din("dl", [128, 256])
    biasT = din("biasT", [NH, 128, 5 * T])
    ident_d = din("ident", [128, 128])
    w_sc_in = din("w_sc_in", [D, 3 * D])
    w_sc_out = din("w_sc_out", [D, D])
    w_kv = din("w_kv", [D, 2 * D])
    w_q = din("w_q", [D, D])
    w_o = din("w_o", [D, D])
    w_up = din("w_up", [2, D, 2 * DFF])
    w_down = din("w_down", [2, DFF, D])
    w_pg = din("w_pg", [2, D, D])
    w_pp = din("w_pp", [2, 256, D])
    outT = nc.dram_tensor("outT", [D, S], F32, kind="ExternalOutput").ap()

    panels = []
    pidx = {}

    def addp(key, src, k0, kcp, n0, pw):
        pidx[key] = len(panels)
        panels.append(dict(src=src, k0=k0, kcp=kcp, n0=n0, pw=pw))

    def add_ffn(l):
        seen_p = []
        for j in range(FC):
            for pn in (j // 4, (FC + j) // 4):
                if pn not in seen_p:
                    seen_p.append(pn)
                    addp(("up", l, pn), w_up[l], 0, 8, pn * 512, 512)
        for nh in range(2):
            for kg, (k0, kcp) in enumerate(((0, 8), (8, 8), (16, 6))):
                addp(("down", l, nh, kg), w_down[l], k0, kcp, nh * 512, 512)

    def add_ple(l):
        for pn in range(2):
            addp(("pg", l, pn), w_pg[l], 0, 8, pn * 512, 512)
        addp(("pp", l), w_pp[l], 0, 2, 0, 1024)

    for pn in (0, 2, 4, 1, 3, 5):
        addp(("scin", pn), w_sc_in, 0, 8, pn * 512, 512)
    for pn in range(2):
        addp(("scout", pn), w_sc_out, 0, 8, pn * 512, 512)
    add_ffn(0)
    add_ple(0)
    for pn in range(4):
        addp(("kv", pn), w_kv, 0, 8, pn * 512, 512)
    for pn in range(2):
        addp(("q", pn), w_q, 0, 8, pn * 512, 512)
    for pn in range(2):
        addp(("o", pn), w_o, 0, 8, pn * 512, 512)
    add_ffn(1)
    add_ple(1)
    NP = len(panels)

    wpan = nc.dram_tensor("wpan", [NP, 128, SLOT], BF16, kind="Internal").ap()
    KTd = nc.dram_tensor("KTd", [NH, 128, S], BF16, kind="Internal").ap()
    Vd = nc.dram_tensor("Vd", [NH, 128, S], BF16, kind="Internal").ap()
    expBd = nc.dram_tensor("expBd", [NH, 128, 5 * T], BF16, kind="Internal").ap()

    es = ExitStack()
    with es:
        P = Prog(nc, es)

        def sb(name, shape, dt):
            return es.enter_context(nc.sbuf_tensor(name, list(shape), dt))

        xbuf = [sb("x%d" % i, [128, KC, T], F32) for i in range(2)]
        x_t = [[TL("x%d_%d" % (i, c)) for c in range(KC)] for i in range(2)]
        ybuf = sb("ybuf", [128, KC, T], F32)
        y_t = [TL("y%d" % c) for c in range(KC)]
        hb = sb("hb", [128, KC, T], BF16)
        hb_t = [TL("hb_%d" % c) for c in range(KC)]
        hb2 = sb("hb2", [128, KC, T], BF16)
        hb2_t = [TL("hb2_%d" % c) for c in range(KC)]
        sq = sb("sq", [128, KC, T], BF16)
        sq_t = [TL("sq%d" % c) for c in range(KC)]
        actb = sb("actb", [128, FC * T], BF16)
        actb3 = actb[:].rearrange("p (c t) -> p c t", c=FC)
        act_t = [TL("act%d" % c) for c in range(FC)]
        vst = actb[:, 0:4 * 1024].rearrange("p (s f) -> p s f", s=4)
        vst_t = act_t[0:8]
        qz = [sb("qz%d" % m, [128, NH, T], BF16) for m in range(2)]
        qt_t = TL("qt")
        ring = sb("ring", [128, NSLOT * SLOT], BF16)
        slot_t = [TL("slot%d" % i) for i in range(NSLOT)]
        slot_sem = [P.newsem("slot%d" % i) for i in range(NSLOT)]
        tmps = [BK(sb("tmp%d" % i, [128, T + 2], F32)[:], TL("tmp%d" % i)) for i in range(NTMP)]
        ets = [BK(sb("et%d" % i, [128, 2, T], BF16), TL("et%d" % i)) for i in range(NE)]
        for e_ in ets:
            e_.tm = [TL(e_.t.name + "m0"), TL(e_.t.name + "m1")]
        pst = sb("pst", [128, 2, T], F32)
        pst_t = TL("pst")
        pb = sb("pb", [128, 2, T], BF16)
        pb_t = TL("pb")
        ones = sb("ones", [128, 128], BF16)
        ones32 = sb("ones32", [128, 128], F32)
        warm = sb("warm", [128, 2], F32)
        warm_t = TL("warm")
        ident32 = sb("ident32", [128, 128], F32)
        ident = sb("identb", [128, 128], BF16)
        accD = [BK(sb("acc%d" % a_, [128, 2, T], F32), TL("acc%d" % a_)) for a_ in range(2)]
        sqos = [BK(sb("sqo%d" % a_, [128, T], BF16)[:], TL("sqo%d" % a_)) for a_ in range(2)]
        ones_t = TL("ones", const=True)
        gv = sb("gvs", [128, 13 * 8], F32)
        scw = sb("scws", [128, 24], F32)
        fcw = sb("fcws", [128, 2 * 3 * 44], F32)
        gsub = sb("gsubs", [128, 1], F32)
        cb = sb("cbs", [128, NH], F32)
        ncb = sb("ncbs", [128, NH], F32)
        dl = sb("dls", [128, 256], F32)
        lt = sb("lt", [128, 128], F32)
        ls = sb("lss", [128, 4], F32)
        nlam = sb("nlam", [128, 1], F32)
        cst_t = TL("consts", const=True)
        halo_sc = sb("halo_sc", [128, KC, 2], F32)
        halo_sc_t = TL("halo_sc")
        halo_f = sb("halo_f", [128, 2 * 44, 2], F32)
        halo_f_t = TL("halo_f")
        banks = [BK(es.enter_context(nc.psum_tensor("ps%d" % i, [128, T], F32))[:], TL("ps%d" % i)) for i in range(8)]

        rr = {"s3": 0, "bank": 0, "sbank": 0, "tmp": 0, "et": 0, "slot": 0, "cv": 0, "io": 0}

        pinned = set()

        def bank():
            while True:
                b = banks[rr["bank"] % 8]
                rr["bank"] += 1
                if b.t.name not in pinned:
                    return b

        stat = {"bank": None, "queue": [], "n": 0}

        def stat_begin():
            b = bank()
            pinned.add(b.t.name)
            stat["bank"] = b
            stat["queue"] = []
            stat["n"] = 0

        def _stat_mm(c, last):
            b = stat["bank"]
            mm(b.ap, ones[:], sq[:, c, :], stat["n"] == 0, last, [ones_t, sq_t[c]], [b.t], signal=last)
            stat["n"] += 1

        def stat_push(c):
            if stat["bank"] is not None:
                stat["queue"].append(c)

        def stat_flush():
            if stat["bank"] is None:
                return
            for c in stat["queue"]:
                _stat_mm(c, False)
            stat["queue"] = []

        def stat_finish():
            q_ = stat["queue"]
            for k_, c in enumerate(q_):
                _stat_mm(c, k_ == len(q_) - 1)
            b = stat["bank"]
            assert stat["n"] == KC and q_, "stat accumulation incomplete"
            pinned.discard(b.t.name)
            stat["bank"] = None
            stat["queue"] = []
            rs = tmp()
            act(rs.ap[:, 0:T], b.ap, AF.Ln, [b.t], [rs.t], bias=EPS, scale=1.0 / D)
            act(rs.ap[:, 0:T], rs.ap[:, 0:T], AF.Exp, [rs.t], [rs.t], scale=-0.5)
            return rs

        def sbank():
            b = banks[rr["sbank"] % 4]
            rr["sbank"] += 1
            return b

        def tmp():
            b = tmps[rr["tmp"] % NTMP]
            rr["tmp"] += 1
            return b

        def etile():
            b = ets[rr["et"] % NE]
            rr["et"] += 1
            return b

        cv_sems = [P.newsem("cv%d" % i) for i in range(8)]
        io_sems = [P.newsem("io%d" % i) for i in range(8)]
        out_sems = [P.newsem("out%d" % i) for i in range(2)]

        pio_sems = [P.newsem("pio%d" % i) for i in range(8)]
        rr["pio"] = 0

        def io_sem():
            s_ = io_sems[rr["io"] % 8]
            rr["io"] += 1
            return s_

        def pio_sem():
            s_ = pio_sems[rr["pio"] % 8]
            rr["pio"] += 1
            return s_

        def mm(out, lhsT, rhs, start, stop, reads, writes, signal=False):
            P.op("pe", lambda e: e.matmul(out, lhsT=lhsT, rhs=rhs, start=start, stop=stop), reads, writes, signal)

        def act(out, in_, func, reads, writes, bias=None, scale=None):
            kw = {}
            if bias is not None:
                kw["bias"] = bias
            if scale is not None:
                kw["scale"] = scale
            P.op("act", lambda e: e.activation(out=out, in_=in_, func=func, **kw), reads, writes)

        def tt(eng, out, in0, in1, op, reads, writes):
            P.op(eng, lambda e: e.tensor_tensor(out=out, in0=in0, in1=in1, op=op), reads, writes)

        def stt(eng, out, in0, scalar, in1, op0, op1, reads, writes):
            P.op(eng, lambda e: e.scalar_tensor_tensor(out=out, in0=in0, scalar=scalar, in1=in1, op0=op0, op1=op1),
                 reads, writes)

        def cp(eng, out, in_, reads, writes):
            P.op(eng, lambda e: e.tensor_copy(out=out, in_=in_), reads, writes)

        pan_t = [TL("pan%d" % i) for i in range(NP)]
        conv_done = [0]

        def emit_conv(upto):
            while conv_done[0] < min(upto, NP):
                i = conv_done[0]
                pd = panels[i]
                kcp, pw = pd["kcp"], pd["pw"]
                src = pd["src"][pd["k0"] * 128:(pd["k0"] + kcp) * 128, pd["n0"]:pd["n0"] + pw]
                src = src.rearrange("(kc p) n -> p kc n", p=128)
                dst = wpan[i][:, 0:kcp * pw].rearrange("p (kc n) -> p kc n", kc=kcp)
                s_ = cv_sems[rr["cv"] % 8]
                rr["cv"] += 1
                P.dma("pool", dst, src, [], [pan_t[i]], s_)
                conv_done[0] += 1

        slot_gen = [0] * NSLOT

        def load_slot(src_ap, n, src_tiles):
            k = rr["slot"] % NSLOT
            rr["slot"] += 1
            dst = ring[:, k * SLOT:k * SLOT + n]
            P.dma("sp", dst, src_ap, src_tiles, [slot_t[k]], slot_sem[k])
            b = BK(dst, slot_t[k])
            b.k = k
            b.gen = rr["slot"]
            slot_gen[k] = b.gen
            return b

        def load_panel(key):
            i = pidx[key]
            emit_conv(i + 10)
            pd = panels[i]
            n = pd["kcp"] * pd["pw"]
            b = load_slot(wpan[i][:, 0:n], n, [pan_t[i]])
            b2 = BK(b.ap.rearrange("p (kc n) -> p kc n", kc=pd["kcp"]), b.t)
            b2.k = b.k
            b2.gen = b.gen
            return b2

        s0 = io_sem()
        for dst, src in ((gv, gv_d), (scw, scw_d), (fcw, fcw_d), (gsub, gsub_d), (cb, cb_d), (dl, dl_d)):
            P.dma("sp", dst[:], src, [], [cst_t], s0)
        P.dma("sp", ident32[:], ident_d, [], [ones_t], s0)
        P.op("dve", lambda e: e.memset(ones[:], 1.0), [], [ones_t])
        P.op("dve", lambda e: e.memset(warm[:], 1.0), [], [warm_t])
        P.op("dve", lambda e: e.tensor_copy(out=ident[:], in_=ident32[:]), [ones_t], [ones_t])
        P.op("dve", lambda e: e.memset(ones32[:], 1.0), [], [ones_t])
        P.op("dve", lambda e: e.memset(qz[0][:], 0.0), [], [qt_t])
        P.op("dve", lambda e: e.memset(qz[1][:], 0.0), [], [qt_t])
        P.op("dve", lambda e: e.memset(halo_sc[:], 0.0), [], [halo_sc_t])
        P.op("dve", lambda e: e.memset(halo_f[:], 0.0), [], [halo_f_t])
        tt("dve", lt[:, 0:64], dl[:, 0:64], dl[:, 64:128], ALU.mult, [cst_t], [cst_t])
        tt("dve", lt[:, 64:128], dl[:, 128:192], dl[:, 192:256], ALU.mult, [cst_t], [cst_t])
        P.op("dve", lambda e: e.reduce_sum(out=ls[:, 0:1], in_=lt[:, 0:64], axis=AX.X), [cst_t], [cst_t])
        P.op("dve", lambda e: e.reduce_sum(out=ls[:, 1:2], in_=lt[:, 64:128], axis=AX.X), [cst_t], [cst_t])
        act(ls[:, 2:4], ls[:, 0:2], AF.Exp, [cst_t], [cst_t])
        tt("dve", nlam[:, 0:1], ls[:, 3:4], ls[:, 2:3], ALU.subtract, [cst_t], [cst_t])
        P.op("dve", lambda e: e.tensor_scalar_add(out=nlam[:, 0:1], in0=nlam[:, 0:1], scalar1=-LI), [cst_t], [cst_t])
        P.op("dve", lambda e: e.tensor_scalar_mul(out=gsub[:, 0:1], in0=gsub[:, 0:1], scalar1=(1.0 - LI)), [cst_t], [cst_t])
        P.op("dve", lambda e: e.tensor_scalar_mul(out=ncb[:], in0=cb[:], scalar1=-1.0), [cst_t], [cst_t])
        expB_t = [TL("expB%d" % h) for h in range(NH)]
        P.dma("sp", xbuf[0][:], xT.rearrange("(c p) t -> p c t", p=128)[:, :, 0:T], [], x_t[0], io_sem())
        emit_conv(12)
        for h in range(NH):
            P.dma("pool", expBd[h].rearrange("p (o q) -> p o q", o=5), biasT[h].rearrange("p (o q) -> p o q", o=5),
                  [], [expB_t[h]], pio_sem())

        emit_conv(12)

        KT_dt = TL("KTd")
        V_dt = [TL("Vd%d" % s_) for s_ in range(4)]

        def sumsq_rstd(nchunks, dim):
            b = bank()
            for c in range(nchunks):
                mm(b.ap, ones[:], sq[:, c, :], c == 0, c == nchunks - 1, [ones_t, sq_t[c]], [b.t], signal=(c == nchunks - 1))
            rs = tmp()
            act(rs.ap[:, 0:T], b.ap, AF.Ln, [b.t], [rs.t], bias=EPS, scale=1.0 / dim)
            act(rs.ap[:, 0:T], rs.ap[:, 0:T], AF.Exp, [rs.t], [rs.t], scale=-0.5)
            return rs

        def prenorm(xi, outs, skip_square=False):
            x = xbuf[xi]
            if not skip_square:
                for half in range(2):
                    c0, c1 = 4 * half, 4 * half + 4
                    act(sq[:, c0:c1, :], x[:, c0:c1, :], AF.Square, x_t[xi][c0:c1], sq_t[c0:c1])
            rs = sumsq_rstd(KC, D)
            for gidx, ob, ot, eng in outs:
                for c in range(KC):
                    stt(eng, ob[:, c, :], x[:, c, :], gv[:, gidx * 8 + c:gidx * 8 + c + 1], rs.ap[:, 0:T],
                        ALU.mult, ALU.mult, [x_t[xi][c], rs.t, cst_t], [ot[c]])

        def postnorm_res(xi, gidx, sq_next=False):
            x = xbuf[xi]
            rs = stat_finish() if stat["bank"] is not None else sumsq_rstd(KC, D)
            for c in range(KC):
                tt("dve", ybuf[:, c, :], ybuf[:, c, :], rs.ap[:, 0:T], ALU.mult, [y_t[c], rs.t], [y_t[c]])
                stt("dve", x[:, c, :], ybuf[:, c, :], gv[:, gidx * 8 + c:gidx * 8 + c + 1], x[:, c, :],
                    ALU.mult, ALU.add, [y_t[c], x_t[xi][c], cst_t], [x_t[xi][c]])
                if sq_next:
                    act(sq[:, c, :], x[:, c, :], AF.Square, [x_t[xi][c]], [sq_t[c]])

        def evac_y(oc, b):
            act(sq[:, oc, :], b.ap, AF.Square, [], [sq_t[oc], b.t])
            cp("dve", ybuf[:, oc, :], b.ap, [b.t], [y_t[oc]])
            stat_push(oc)

        def proj8(pkeys, inb, in_t, evac, kouter=False):
            pn = None
            pre = {}
            if kouter:
                pn = load_panel(pkeys[0])
                bs_ = [bank() for _ in range(4)]
                for kc in range(KC):
                    for ocl in range(4):
                        mm(bs_[ocl].ap, pn.ap[:, kc, ocl * 128:(ocl + 1) * 128], inb[:, kc, :], kc == 0, kc == KC - 1,
                           [pn.t, in_t[kc]], [bs_[ocl].t], signal=(kc == KC - 1))
                for ocl in range(4):
                    pre[ocl] = bs_[ocl]
            for oc in range(8):
                if oc in pre:
                    stat_flush()
                    evac(oc, pre[oc])
                    continue
                if oc % 4 == 0:
                    pn = load_panel(pkeys[oc // 4])
                b = bank()
                for kc in range(KC):
                    mm(b.ap, pn.ap[:, kc, (oc % 4) * 128:(oc % 4 + 1) * 128], inb[:, kc, :], kc == 0, kc == KC - 1,
                       [pn.t, in_t[kc]], [b.t], signal=(kc == KC - 1))
                stat_flush()
                evac(oc, b)

        def conv_taps(src, acc, wt, wbase, wstride, reads_src):
            act(acc.ap[:, 0:T], src.ap[:, 2:T + 2], AF.Copy, [src.t, cst_t], [acc.t],
                scale=wt[:, wbase + 2 * wstride:wbase + 2 * wstride + 1])
            stt("dve", acc.ap[:, 0:T], src.ap[:, 1:T + 1], wt[:, wbase + wstride:wbase + wstride + 1], acc.ap[:, 0:T],
                ALU.mult, ALU.add, [src.t, acc.t, cst_t], [acc.t])
            stt("dve", acc.ap[:, 0:T], src.ap[:, 0:T], wt[:, wbase:wbase + 1], acc.ap[:, 0:T],
                ALU.mult, ALU.add, [src.t, acc.t, cst_t], [acc.t])

        def sc_mixer(xi):
            P.phase = "scmix"
            prenorm(xi, [(0, hb, hb_t, "dve")])
            pn = {}
            for c in range(KC):
                if c % 4 == 0:
                    for g_ in range(3):
                        pn[g_] = load_panel(("scin", 2 * g_ + c // 4))
                bs = []
                if c == 0:
                    bs = [bank() for _ in range(3)]
                    for kc in range(KC):
                        for g_ in range(3):
                            mm(bs[g_].ap, pn[g_].ap[:, kc, 0:128], hb[:, kc, :], kc == 0, kc == KC - 1,
                               [pn[g_].t, hb_t[kc]], [bs[g_].t], signal=(kc == KC - 1))
                else:
                    for g_ in range(3):
                        b = bank()
                        for kc in range(KC):
                            mm(b.ap, pn[g_].ap[:, kc, (c % 4) * 128:(c % 4 + 1) * 128], hb[:, kc, :], kc == 0, kc == KC - 1,
                               [pn[g_].t, hb_t[kc]], [b.t], signal=(kc == KC - 1))
                        bs.append(b)
                gcs = tmp()
                act(gcs.ap[:, 0:T], bs[1].ap, AF.Copy, [bs[1].t], [gcs.t])
                pr = tmp()
                cp("pool", pr.ap[:, 0:2], halo_sc[:, c, :], [halo_sc_t], [pr.t])
                tt("dve", pr.ap[:, 2:T + 2], gcs.ap[:, 0:T], bs[2].ap, ALU.mult, [gcs.t, bs[2].t], [pr.t])
                cp("pool", halo_sc[:, c, :], pr.ap[:, T:T + 2], [pr.t], [halo_sc_t])
                acc = tmp()
                conv_taps(pr, acc, scw, c, 8, None)
                tt("dve", hb2[:, c, :], acc.ap[:, 0:T], bs[0].ap, ALU.mult, [acc.t, bs[0].t], [hb2_t[c]])
            stat_begin()
            proj8([("scout", 0), ("scout", 1)], hb2, hb2_t, evac_y)
            postnorm_res(xi, 1, sq_next=True)

        def ffn(xi, l):
            P.phase = "ffn_up"
            prenorm(xi, [(6 * l + 2, hb, hb_t, "dve")], skip_square=True)
            pcur = {}
            pre = {}
            for cidx in (0, FC, 1, FC + 1):
                pnum = cidx // 4
                if pnum not in pcur:
                    pcur[pnum] = load_panel(("up", l, pnum))
                pre[cidx] = bank()
            for kc in range(KC):
                for cidx in (0, FC, 1, FC + 1):
                    pn = pcur[cidx // 4]
                    mm(pre[cidx].ap, pn.ap[:, kc, (cidx % 4) * 128:(cidx % 4 + 1) * 128], hb[:, kc, :], kc == 0, kc == KC - 1,
                       [pn.t, hb_t[kc]], [pre[cidx].t], signal=(kc == KC - 1))
            for j in range(FC):
                accs = []
                for which, cidx in ((0, j), (1, FC + j)):
                    if cidx in pre:
                        b = pre[cidx]
                    else:
                        pnum = cidx // 4
                        if pnum not in pcur or slot_gen[pcur[pnum].k] != pcur[pnum].gen:
                            pcur[pnum] = load_panel(("up", l, pnum))
                        pn = pcur[pnum]
                        b = bank()
                        for kc in range(KC):
                            mm(b.ap, pn.ap[:, kc, (cidx % 4) * 128:(cidx % 4 + 1) * 128], hb[:, kc, :], kc == 0, kc == KC - 1,
                               [pn.t, hb_t[kc]], [b.t], signal=(kc == KC - 1))
                    yc = tmp()
                    hidx = l * 44 + cidx
                    cp("pool", yc.ap[:, 0:2], halo_f[:, hidx, :], [halo_f_t], [yc.t])
                    act(yc.ap[:, 2:T + 2], b.ap, AF.Copy, [b.t], [yc.t])
                    cp("pool", halo_f[:, hidx, :], yc.ap[:, T:T + 2], [yc.t], [halo_f_t])
                    acc = tmp()
                    conv_taps(yc, acc, fcw, l * 132 + cidx, 44, None)
                    accs.append(acc)
                sg = tmp()
                act(sg.ap[:, 0:T], accs[0].ap[:, 0:T], AF.Silu, [accs[0].t], [sg.t])
                tt("dve", actb3[:, j, :], sg.ap[:, 0:T], accs[1].ap[:, 0:T], ALU.mult, [sg.t, accs[1].t], [act_t[j]])
                if j == FC - 1:
                    act(warm[:, 0:1], ones32[:, 0:1], AF.Ln, [ones_t], [warm_t])
            P.phase = "ffn_down"
            stat_begin()
            for nh in range(2):
                bs = [bank() for _ in range(4)]
                for kg, (k0, kcp) in enumerate(((0, 8), (8, 8), (16, 6))):
                    pn = load_panel(("down", l, nh, kg))
                    for ocl in range(4):
                        for kcl in range(kcp):
                            first = (kg == 0 and kcl == 0)
                            last = (kg == 2 and kcl == kcp - 1)
                            mm(bs[ocl].ap, pn.ap[:, kcl, ocl * 128:(ocl + 1) * 128], actb3[:, k0 + kcl, :], first, last,
                               [pn.t, act_t[k0 + kcl]], [bs[ocl].t], signal=(kcl == kcp - 1))
                        stat_flush()
                for ocl in range(4):
                    evac_y(nh * 4 + ocl, bs[ocl])
            postnorm_res(xi, 6 * l + 3, sq_next=True)

        def ple(xi, l, t0):
            P.phase = "ple"
            prenorm(xi, [(6 * l + 4, hb, hb_t, "dve")], skip_square=True)
            P.dma("sp", pst[:], pT[l].rearrange("(c p) t -> p c t", p=128)[:, :, t0:t0 + T], [], [pst_t], io_sem())
            act(pb[:], pst[:], AF.Copy, [pst_t], [pb_t])
            pp = load_panel(("pp", l))
            pg = None
            stat_begin()
            for oc in range(8):
                if oc % 4 == 0:
                    pg = load_panel(("pg", l, oc // 4))
                ba = bank()
                for kc in range(KC):
                    mm(ba.ap, pg.ap[:, kc, (oc % 4) * 128:(oc % 4 + 1) * 128], hb[:, kc, :], kc == 0, kc == KC - 1,
                       [pg.t, hb_t[kc]], [ba.t], signal=(kc == KC - 1))
                bb = bank()
                for kc in range(2):
                    mm(bb.ap, pp.ap[:, kc, oc * 128:(oc + 1) * 128], pb[:, kc, :], kc == 0, kc == 1,
                       [pp.t, pb_t], [bb.t], signal=(kc == 1))
                stat_flush()
                sg = tmp()
                act(sg.ap[:, 0:T], ba.ap, AF.Sigmoid, [ba.t], [sg.t])
                tt("dve", ybuf[:, oc, :], sg.ap[:, 0:T], bb.ap, ALU.mult, [sg.t, bb.t], [y_t[oc]])
                act(sq[:, oc, :], ybuf[:, oc, :], AF.Square, [y_t[oc]], [sq_t[oc]])
                stat_push(oc)
                if oc == 7:
                    act(warm[:, 0:1], ones32[:, 0:1], AF.Ln, [ones_t], [warm_t])
            postnorm_res(xi, 6 * l + 5, sq_next=(l == 0))

        def attention(xi, i, t0):
            P.phase = "kvq"
            prenorm(xi, [(12, hb2, hb2_t, "dve"), (6, hb, hb_t, "dve")], skip_square=True)
            kst = sq

            def evac_k(oc, b):
                act(kst[:, oc, :], b.ap, AF.Copy, [b.t], [sq_t[oc]])

            proj8([("kv", 0), ("kv", 1)], hb2, hb2_t, evac_k, kouter=True)
            P.dma("pool", KTd.rearrange("h p t -> p h t")[:, :, t0:t0 + T], kst[:], sq_t, [KT_dt], pio_sem())
            for nh in range(2):
                pn = load_panel(("kv", 2 + nh))
                for s_ in range(4):
                    b = bank()
                    for kc in range(KC):
                        mm(b.ap, hb2[:, kc, s_ * 128:(s_ + 1) * 128], pn.ap[:, kc, :], kc == 0, kc == KC - 1,
                           [pn.t, hb2_t[kc]], [b.t], signal=(kc == KC - 1))
                    if s_ % 2 == 0:
                        cp("dve", vst[:, s_, nh * 512:(nh + 1) * 512], b.ap, [b.t], vst_t)
                    else:
                        act(vst[:, s_, nh * 512:(nh + 1) * 512], b.ap, AF.Copy, [b.t], vst_t)
            for s_ in range(4):
                dstv = Vd.rearrange("h p (b e) -> p b h e", e=128)[:, 4 * i + s_, :, :]
                P.dma("pool", dstv, vst[:, s_, :].rearrange("p (h e) -> p h e", e=128), vst_t, [V_dt[s_]], pio_sem())

            def evac_q(oc, b):
                act(qz[0][0:64, oc, :], b.ap[0:64, :], AF.Copy, [b.t], [qt_t], scale=0.125)
                act(qz[1][64:128, oc, :], b.ap[64:128, :], AF.Copy, [b.t], [qt_t], scale=0.125)

            proj8([("q", 0), ("q", 1)], hb, hb_t, evac_q)

            nk = 4 * (i + 1)
            ao = hb2
            pending = []

            def run_pending(jnow):
                keep = []
                for trig, fn_ in pending:
                    if trig <= jnow:
                        fn_()
                    else:
                        keep.append((trig, fn_))
                pending[:] = keep

            for hh in range(NH):
                P.phase = "attn"
                ktp = load_slot(KTd[hh][:, 0:nk * 128], nk * 128, [KT_dt])
                vpb = load_slot(Vd[hh][:, 0:nk * 128], nk * 128, V_dt)
                vp = vpb.ap.rearrange("p (b e) -> p b e", e=128)
                ebb = load_slot(expBd[hh], 5 * T, [expB_t[hh]])
                eb = ebb.ap.rearrange("p (o q) -> p o q", o=5)
                par = hh % 2
                O = banks[4 + 2 * par:6 + 2 * par]
                acc = accD[par]

                D0 = banks[3]

                def emit_qk_m(j, m, ktp=ktp, hh=hh, ebb=ebb, eb=eb):
                    o_ = T * i - 128 * j
                    special = o_ <= 128
                    qs = max(0, -o_)
                    sb_ = banks[rr["s3"] % 3]
                    rr["s3"] += 1
                    mm(sb_.ap[:, qs:T], ktp.ap[:, j * 128:(j + 1) * 128], qz[m][:, hh, qs:T],
                       True, not special, [ktp.t, qt_t], [sb_.t], signal=(not special))
                    if special:
                        oi = (o_ + 384) // 128
                        mm(sb_.ap[:, qs:T], ident[:], eb[:, oi, qs:T], False, True, [ones_t, ebb.t], [sb_.t], signal=True)
                    return sb_

                Sb = [emit_qk_m(0, 0), emit_qk_m(0, 1)]
                for j in range(nk):
                    o_ = T * i - 128 * j
                    qs = max(0, -o_)
                    E = etile()
                    Sn = [None, None]
                    for m in range(2):
                        if j + 1 < nk:
                            Sn[m] = emit_qk_m(j + 1, m)
                        if o_ <= 128:
                            act(E.ap[:, m, qs:T], Sb[m].ap[:, qs:T], AF.Exp, [Sb[m].t], [E.tm[m]])
                        else:
                            act(E.ap[:, m, :], Sb[m].ap, AF.Exp, [Sb[m].t, cst_t], [E.tm[m]], bias=cb[:, hh:hh + 1])
                    for m in range(2):
                        mm(O[m].ap[:, qs:T], vp[:, j, :], E.ap[:, m, qs:T], j == 0, j == nk - 1, [vpb.t, E.tm[m]], [O[m].t],
                           signal=True)
                    mm(D0.ap[:, qs:T], ones[:], E.ap[:, 0, qs:T], j == 0, j == nk - 1, [ones_t, E.tm[0]], [D0.t], signal=True)
                    if j == 0:
                        cp("dve", acc.ap[:, 1, :], E.ap[:, 1, :], [E.tm[1]], [acc.t])
                    else:
                        tt("dve", acc.ap[:, 1, qs:T], acc.ap[:, 1, qs:T], E.ap[:, 1, qs:T], ALU.add, [acc.t, E.tm[1]], [acc.t])
                    Sb = Sn
                    run_pending(j)
                P.phase = "attn_fin"
                d1_ = banks[rr["s3"] % 3]
                rr["s3"] += 1
                mm(d1_.ap, ones32[:], acc.ap[:, 1, :], True, True, [ones_t, acc.t], [d1_.t], signal=True)
                Db = [D0, d1_]
                r0 = tmp()
                r1 = tmp()
                rs = tmp()
                sqo = sqos[par]
                for r_, d_ in ((r0, Db[0]), (r1, Db[1])):
                    act(r_.ap[:, 0:T], d_.ap, AF.Ln, [d_.t], [r_.t])
                    act(r_.ap[:, 0:T], r_.ap[:, 0:T], AF.Exp, [r_.t], [r_.t], scale=-1.0)

                def fin_a(r0=r0, r1=r1, O=O):
                    tt("dve", r0.ap[:, 0:T], O[0].ap, r0.ap[:, 0:T], ALU.mult, [O[0].t, r0.t], [r0.t])
                    tt("dve", r1.ap[:, 0:T], O[1].ap, r1.ap[:, 0:T], ALU.mult, [O[1].t, r1.t], [r1.t])
                    stt("dve", r0.ap[:, 0:T], r1.ap[:, 0:T], nlam[:, 0:1], r0.ap[:, 0:T], ALU.mult, ALU.add,
                        [r0.t, r1.t, cst_t], [r0.t])

                def fin_b(r0=r0, sqo=sqo, O=O):
                    act(sqo.ap, r0.ap[:, 0:T], AF.Square, [r0.t], [sqo.t])
                    mm(O[0].ap, ones[:], sqo.ap, True, True, [ones_t, sqo.t], [O[0].t], signal=True)

                def fin_c(rs=rs, O=O):
                    act(rs.ap[:, 0:T], O[0].ap, AF.Ln, [O[0].t], [rs.t], bias=EPS, scale=1.0 / 128)
                    act(rs.ap[:, 0:T], rs.ap[:, 0:T], AF.Exp, [rs.t], [rs.t], scale=-0.5)

                def fin_d(r0=r0, rs=rs, hh=hh):
                    stt("dve", ao[:, hh, :], r0.ap[:, 0:T], gsub[:, 0:1], rs.ap[:, 0:T], ALU.mult, ALU.mult,
                        [r0.t, rs.t, cst_t], [hb2_t[hh]])

                pending.append((0, fin_a))
                pending.append((min(2, nk - 1), fin_b))
                pending.append((min(4, nk - 1), fin_c))
                pending.append((min(6, nk - 1), fin_d))
            run_pending(10 ** 9)
            P.phase = "wo"
            stat_begin()
            proj8([("o", 0), ("o", 1)], ao, hb2_t, evac_y)
            postnorm_res(xi, 7, sq_next=True)

        for i in range(n_tiles):
            xi = i % 2
            t0 = i * T
            if i > 0:
                P.dma("sp", xbuf[xi][:], xT.rearrange("(c p) t -> p c t", p=128)[:, :, t0:t0 + T], [], x_t[xi], io_sem())
            sc_mixer(xi)
            if stage >= 2:
                ffn(xi, 0)
            if stage >= 3:
                ple(xi, 0, t0)
            if stage >= 4:
                attention(xi, i, t0)
            if stage >= 5:
                ffn(xi, 1)
            if stage >= 6:
                ple(xi, 1, t0)
            P.dma("pool", outT.rearrange("(c p) t -> p c t", p=128)[:, :, t0:t0 + T], xbuf[xi][:], x_t[xi], [], out_sems[xi])
        P.final_wait("pool", out_sems)

        with nc.Block() as block:
            @block.tensor
            def _(e):
                P.replay("pe", e)

            @block.scalar
            def _(e):
                P.replay("act", e)

            @block.vector
            def _(e):
                P.replay("dve", e)

            @block.gpsimd
            def _(e):
                P.replay("pool", e)

            @block.sync
            def _(e):
                P.replay("sp", e)
    build_program.last_prog = P
    return nc


def _rel_bucket_np(dist):
    max_exact = 16
    d = np.maximum(dist, 1).astype(np.float32)
    large = max_exact + (np.log(d / np.float32(max_exact)) / np.float32(math.log(128 / max_exact))
                         * np.float32(32 - max_exact)).astype(np.int32)
    large = np.minimum(large, 31)
    return np.where(dist < max_exact, dist, large)


def _host_inputs(inp, b):
    f = np.float32
    x = np.asarray(inp["x"], f)
    p = np.asarray(inp["p"], f)
    m = {}
    m["xT"] = np.ascontiguousarray(x[b].T)
    m["pT"] = np.ascontiguousarray(p[:, b].transpose(0, 2, 1))
    gl = []
    for l in range(2):
        for nm in ("g_pre_mix", "g_post_mix", "g_pre_ffn", "g_post_ffn", "g_pre_ple", "g_post_ple"):
            gl.append(np.asarray(inp[nm], f)[l])
    gl.append(np.asarray(inp["g_kv"], f))
    g = np.stack(gl, 0).reshape(13, 8, 128).transpose(2, 0, 1).reshape(128, 104)
    m["gv"] = np.ascontiguousarray(g)
    scw = np.asarray(inp["w_sc_conv"], f)[0].reshape(3, 8, 128).transpose(2, 0, 1).reshape(128, 24)
    m["scw"] = np.ascontiguousarray(scw)
    fcw = np.asarray(inp["w_ffn_conv"], f).reshape(2, 3, 44, 128).transpose(3, 0, 1, 2).reshape(128, 264)
    m["fcw"] = np.ascontiguousarray(fcw)
    m["gsub"] = np.ascontiguousarray(np.asarray(inp["g_subln"], f)[0].reshape(128, 1))
    rb = np.asarray(inp["rel_bias"], f)
    m["cb"] = np.ascontiguousarray(np.broadcast_to(rb[31:32, :], (128, NH)))
    m["dl"] = np.ascontiguousarray(np.broadcast_to(np.asarray(inp["diff_lambda"], f)[0].reshape(1, 256), (128, 256)))
    kk = np.arange(128)[:, None]
    qq = np.arange(T)[None, :]
    bt = np.empty((NH, 128, 5, T), f)
    for oi in range(5):
        o_ = -384 + 128 * oi
        d = o_ + qq - kk
        bk = _rel_bucket_np(np.maximum(d, 0))
        vals = rb[bk]
        vals = np.where((d >= 0)[:, :, None], vals, f(NEG))
        bt[:, :, oi, :] = vals.transpose(2, 0, 1)
    m["biasT"] = np.ascontiguousarray(bt.reshape(NH, 128, 5 * T))
    m["ident"] = np.eye(128, dtype=f)
    m["w_sc_in"] = np.ascontiguousarray(np.asarray(inp["w_sc_in"], f)[0])
    m["w_sc_out"] = np.ascontiguousarray(np.asarray(inp["w_sc_out"], f)[0])
    m["w_kv"] = np.ascontiguousarray(np.asarray(inp["w_kv"], f))
    m["w_q"] = np.ascontiguousarray(np.asarray(inp["w_q"], f)[0])
    m["w_o"] = np.ascontiguousarray(np.asarray(inp["w_o"], f)[0])
    m["w_up"] = np.ascontiguousarray(np.asarray(inp["w_ffn_up"], f))
    m["w_down"] = np.ascontiguousarray(np.asarray(inp["w_ffn_down"], f))
    m["w_pg"] = np.ascontiguousarray(np.asarray(inp["w_ple_gate"], f))
    m["w_pp"] = np.ascontiguousarray(np.asarray(inp["w_ple_proj"], f))
    return m


def kernel(**inputs):
    nc = build_program()
    shared = None
    in_maps = []
    for b in range(8):
        m = _host_inputs(inputs, b) if shared is None else None
        if shared is None:
            shared = m
        else:
            f = np.float32
            m = dict(shared)
            m["xT"] = np.ascontiguousarray(np.asarray(inputs["x"], f)[b].T)
            m["pT"] = np.ascontiguousarray(np.asarray(inputs["p"], f)[:, b].transpose(0, 2, 1))
        in_maps.append(m)
    res = run_bass_kernel_spmd(nc, in_maps, core_ids=list(range(8)))
    out = np.stack([np.asarray(r["outT"]).T for r in res.results], 0)
    return np.ascontiguousarray(out.astype(np.float32))
```

```python
import math
from contextlib import ExitStack

import numpy as np
import concourse.bass as bass
import concourse.mybir as mybir
from concourse.bass_utils import run_bass_kernel_spmd

F32 = mybir.dt.float32
BF16 = mybir.dt.bfloat16
AF = mybir.ActivationFunctionType
ALU = mybir.AluOpType
AX = mybir.AxisListType

D = 1024
S = 4096
T = 512
NT = S // T
KC = 8
DFF = 2816
FC = 22
NH = 8
EPS = 1e-6
LI = 0.8 - 0.6 * math.exp(-0.3 * 1)
NEG = -30000.0
SLOT = 4096
NSLOT = 6
NTMP = 10
NE = 4
N_TILES = NT


class Sem:
    def __init__(self, h):
        self.h = h
        self.count = 0


class TL:
    def __init__(self, name, const=False):
        self.name = name
        self.w = {}
        self.r = {}
        self.const = const


class BK:
    def __init__(self, ap, t):
        self.ap = ap
        self.t = t


ENG = ("pe", "act", "dve", "pool", "sp")


class Prog:
    def __init__(self, nc, es):
        self.nc = nc
        self.es = es
        self.q = {k: [] for k in ENG}
        self.seen = {k: {} for k in ENG}
        self.esem = {k: self.newsem("e_" + k) for k in ("pe", "act", "dve", "pool")}
        self.phase = "init"
        self.phases = {k: [] for k in ENG}

    def newsem(self, name):
        return Sem(self.es.enter_context(self.nc.semaphore(name)))

    def _deps(self, eng, reads, writes):
        waits = {}
        seen = self.seen[eng]
        pes = self.esem["pe"] if eng == "pe" else None

        def need(sem, val):
            if sem is pes:
                return
            if seen.get(sem, 0) < val and waits.get(sem, 0) < val:
                waits[sem] = val

        for t in reads:
            for sem, val in t.w.items():
                need(sem, val)
        for t in writes:
            for sem, val in t.w.items():
                need(sem, val)
            for sem, val in t.r.items():
                need(sem, val)
        return waits

    def _commit(self, eng, waits, reads, writes, tok):
        seen = self.seen[eng]
        for sem, val in waits.items():
            seen[sem] = val
        sem, val = tok
        for t in reads:
            if not t.const:
                if t.r.get(sem, 0) < val:
                    t.r[sem] = val
        for t in writes:
            if t.w.get(sem, 0) < val:
                t.w[sem] = val
            t.r = {}

    def op(self, eng, fn, reads=(), writes=(), signal=True):
        waits = self._deps(eng, reads, writes)
        es_ = self.esem[eng]
        if signal:
            es_.count += 1
            tok = (es_, es_.count)
        else:
            tok = (es_, es_.count + 1)
        self._commit(eng, waits, reads, writes, tok)
        self.q[eng].append((list(waits.items()), fn, tok if signal else None, 1))
        self.phases[eng].append(self.phase)

    def dma(self, eng, out, in_, reads, writes, sem):
        waits = self._deps(eng, reads, writes)
        if sem.count > 0 and self.seen[eng].get(sem, 0) < sem.count:
            waits[sem] = max(waits.get(sem, 0), sem.count)
        sem.count += 16
        tok = (sem, sem.count)
        self._commit(eng, waits, reads, writes, tok)
        self.q[eng].append((list(waits.items()), (lambda e: e.dma_start(out=out, in_=in_)), tok, 16))

    def final_wait(self, eng, sems):
        waits = [(s, s.count) for s in sems if s.count > 0]
        self.q[eng].append((waits, None, None, 0))

    def replay(self, name, e):
        for waits, fn, tok, inc in self.q[name]:
            for s, v in waits:
                e.wait_ge(s.h, v)
            if fn is None:
                continue
            ins = fn(e)
            if tok is not None:
                ins.then_inc(tok[0].h, inc)


def build_program(n_tiles=N_TILES, stage=6):
    nc = bass.Bass("TRN2", target_bir_lowering=False)

    def din(name, shape, dt=F32):
        return nc.dram_tensor(name, list(shape), dt, kind="ExternalInput").ap()

    xT = din("xT", [D, S])
    pT = din("pT", [2, 256, S])
    gv_d = din("gv", [128, 13 * 8])
    scw_d = din("scw", [128, 24])
    fcw_d = din("fcw", [128, 2 * 3 * 44])
    gsub_d = din("gsub", [128, 1])
    cb_d = din("cb", [128, NH])
    dl_d = din("dl", [128, 256])
    biasT = din("biasT", [NH, 128, 5 * T])
    ident_d = din("ident", [128, 128])
    w_sc_in = din("w_sc_in", [D, 3 * D])
    w_sc_out = din("w_sc_out", [D, D])
    w_kv = din("w_kv", [D, 2 * D])
    w_q = din("w_q", [D, D])
    w_o = din("w_o", [D, D])
    w_up = din("w_up", [2, D, 2 * DFF])
    w_down = din("w_down", [2, DFF, D])
    w_pg = din("w_pg", [2, D, D])
    w_pp = din("w_pp", [2, 256, D])
    outT = nc.dram_tensor("outT", [D, S], F32, kind="ExternalOutput").ap()

    panels = []
    pidx = {}

    def addp(key, src, k0, kcp, n0, pw):
        pidx[key] = len(panels)
        panels.append(dict(src=src, k0=k0, kcp=kcp, n0=n0, pw=pw))

    def add_ffn(l):
        seen_p = []
        for j in range(FC):
            for pn in (j // 4, (FC + j) // 4):
                if pn not in seen_p:
                    seen_p.append(pn)
                    addp(("up", l, pn), w_up[l], 0, 8, pn * 512, 512)
        for nh in range(2):
            for kg, (k0, kcp) in enumerate(((0, 8), (8, 8), (16, 6))):
                addp(("down", l, nh, kg), w_down[l], k0, kcp, nh * 512, 512)

    def add_ple(l):
        for pn in range(2):
            addp(("pg", l, pn), w_pg[l], 0, 8, pn * 512, 512)
        addp(("pp", l), w_pp[l], 0, 2, 0, 1024)

    for pn in (0, 2, 4, 1, 3, 5):
        addp(("scin", pn), w_sc_in, 0, 8, pn * 512, 512)
    for pn in range(2):
        addp(("scout", pn), w_sc_out, 0, 8, pn * 512, 512)
    add_ffn(0)
    add_ple(0)
    for pn in range(4):
        addp(("kv", pn), w_kv, 0, 8, pn * 512, 512)
    for pn in range(2):
        addp(("q", pn), w_q, 0, 8, pn * 512, 512)
    for pn in range(2):
        addp(("o", pn), w_o, 0, 8, pn * 512, 512)
    add_ffn(1)
    add_ple(1)
    NP = len(panels)

    wpan = nc.dram_tensor("wpan", [NP, 128, SLOT], BF16, kind="Internal").ap()
    KTd = nc.dram_tensor("KTd", [NH, 128, S], BF16, kind="Internal").ap()
    Vd = nc.dram_tensor("Vd", [NH, 128, S], BF16, kind="Internal").ap()
    expBd = nc.dram_tensor("expBd", [NH, 128, 5 * T], BF16, kind="Internal").ap()

    es = ExitStack()
    with es:
        P = Prog(nc, es)

        def sb(name, shape, dt):
            return es.enter_context(nc.sbuf_tensor(name, list(shape), dt))

        xbuf = [sb("x%d" % i, [128, KC, T], F32) for i in range(2)]
        x_t = [[TL("x%d_%d" % (i, c)) for c in range(KC)] for i in range(2)]
        ybuf = sb("ybuf", [128, KC, T], F32)
        y_t = [TL("y%d" % c) for c in range(KC)]
        hb = sb("hb", [128, KC, T], BF16)
        hb_t = [TL("hb_%d" % c) for c in range(KC)]
        hb2 = sb("hb2", [128, KC, T], BF16)
        hb2_t = [TL("hb2_%d" % c) for c in range(KC)]
        sq = sb("sq", [128, KC, T], BF16)
        sq_t = [TL("sq%d" % c) for c in range(KC)]
        actb = sb("actb", [128, FC * T], BF16)
        actb3 = actb[:].rearrange("p (c t) -> p c t", c=FC)
        act_t = [TL("act%d" % c) for c in range(FC)]
        vst = actb[:, 0:4 * 1024].rearrange("p (s f) -> p s f", s=4)
        vst_t = act_t[0:8]
        qz = [sb("qz%d" % m, [128, NH, T], BF16) for m in range(2)]
        qt_t = TL("qt")
        ring = sb("ring", [128, NSLOT * SLOT], BF16)
        slot_t = [TL("slot%d" % i) for i in range(NSLOT)]
        slot_sem = [P.newsem("slot%d" % i) for i in range(NSLOT)]
        tmps = [BK(sb("tmp%d" % i, [128, T + 2], F32)[:], TL("tmp%d" % i)) for i in range(NTMP)]
        ets = [BK(sb("et%d" % i, [128, 2, T], BF16), TL("et%d" % i)) for i in range(NE)]
        for e_ in ets:
            e_.tm = [TL(e_.t.name + "m0"), TL(e_.t.name + "m1")]
        pst = sb("pst", [128, 2, T], F32)
        pst_t = TL("pst")
        pb = sb("pb", [128, 2, T], BF16)
        pb_t = TL("pb")
        ones = sb("ones", [128, 128], BF16)
        ones32 = sb("ones32", [128, 128], F32)
        warm = sb("warm", [128, 2], F32)
        warm_t = TL("warm")
        ident32 = sb("ident32", [128, 128], F32)
        ident = sb("identb", [128, 128], BF16)
        accD = [BK(sb("acc%d" % a_, [128, 2, T], F32), TL("acc%d" % a_)) for a_ in range(2)]
        sqos = [BK(sb("sqo%d" % a_, [128, T], BF16)[:], TL("sqo%d" % a_)) for a_ in range(2)]
        ones_t = TL("ones", const=True)
        gv = sb("gvs", [128, 13 * 8], F32)
        scw = sb("scws", [128, 24], F32)
        fcw = sb("fcws", [128, 2 * 3 * 44], F32)
        gsub = sb("gsubs", [128, 1], F32)
        cb = sb("cbs", [128, NH], F32)
        ncb = sb("ncbs", [128, NH], F32)
        dl = sb("dls", [128, 256], F32)
        lt = sb("lt", [128, 128], F32)
        ls = sb("lss", [128, 4], F32)
        nlam = sb("nlam", [128, 1], F32)
        cst_t = TL("consts", const=True)
        halo_sc = sb("halo_sc", [128, KC, 2], F32)
        halo_sc_t = TL("halo_sc")
        halo_f = sb("halo_f", [128, 2 * 44, 2], F32)
        halo_f_t = TL("halo_f")
        banks = [BK(es.enter_context(nc.psum_tensor("ps%d" % i, [128, T], F32))[:], TL("ps%d" % i)) for i in range(8)]

        rr = {"s3": 0, "bank": 0, "sbank": 0, "tmp": 0, "et": 0, "slot": 0, "cv": 0, "io": 0}

        pinned = set()

        def bank():
            while True:
                b = banks[rr["bank"] % 8]
                rr["bank"] += 1
                if b.t.name not in pinned:
                    return b

        stat = {"bank": None, "queue": [], "n": 0}

        def stat_begin():
            b = bank()
            pinned.add(b.t.name)
            stat["bank"] = b
            stat["queue"] = []
            stat["n"] = 0

        def _stat_mm(c, last):
            b = stat["bank"]
            mm(b.ap, ones[:], sq[:, c, :], stat["n"] == 0, last, [ones_t, sq_t[c]], [b.t], signal=last)
            stat["n"] += 1

        def stat_push(c):
            if stat["bank"] is not None:
                stat["queue"].append(c)

        def stat_flush():
            if stat["bank"] is None:
                return
            for c in stat["queue"]:
                _stat_mm(c, False)
            stat["queue"] = []

        def stat_finish():
            q_ = stat["queue"]
            for k_, c in enumerate(q_):
                _stat_mm(c, k_ == len(q_) - 1)
            b = stat["bank"]
            assert stat["n"] == KC and q_, "stat accumulation incomplete"
            pinned.discard(b.t.name)
            stat["bank"] = None
            stat["queue"] = []
            rs = tmp()
            act(rs.ap[:, 0:T], b.ap, AF.Ln, [b.t], [rs.t], bias=EPS, scale=1.0 / D)
            act(rs.ap[:, 0:T], rs.ap[:, 0:T], AF.Exp, [rs.t], [rs.t], scale=-0.5)
            return rs

        def sbank():
            b = banks[rr["sbank"] % 4]
            rr["sbank"] += 1
            return b

        def tmp():
            b = tmps[rr["tmp"] % NTMP]
            rr["tmp"] += 1
            return b

        def etile():
            b = ets[rr["et"] % NE]
            rr["et"] += 1
            return b

        cv_sems = [P.newsem("cv%d" % i) for i in range(8)]
        io_sems = [P.newsem("io%d" % i) for i in range(8)]
        out_sems = [P.newsem("out%d" % i) for i in range(2)]

        pio_sems = [P.newsem("pio%d" % i) for i in range(8)]
        rr["pio"] = 0

        def io_sem():
            s_ = io_sems[rr["io"] % 8]
            rr["io"] += 1
            return s_

        def pio_sem():
            s_ = pio_sems[rr["pio"] % 8]
            rr["pio"] += 1
            return s_

        def mm(out, lhsT, rhs, start, stop, reads, writes, signal=False):
            P.op("pe", lambda e: e.matmul(out, lhsT=lhsT, rhs=rhs, start=start, stop=stop), reads, writes, signal)

        def act(out, in_, func, reads, writes, bias=None, scale=None):
            kw = {}
            if bias is not None:
                kw["bias"] = bias
            if scale is not None:
                kw["scale"] = scale
            P.op("act", lambda e: e.activation(out=out, in_=in_, func=func, **kw), reads, writes)

        def tt(eng, out, in0, in1, op, reads, writes):
            P.op(eng, lambda e: e.tensor_tensor(out=out, in0=in0, in1=in1, op=op), reads, writes)

        def stt(eng, out, in0, scalar, in1, op0, op1, reads, writes):
            P.op(eng, lambda e: e.scalar_tensor_tensor(out=out, in0=in0, scalar=scalar, in1=in1, op0=op0, op1=op1),
                 reads, writes)

        def cp(eng, out, in_, reads, writes):
            P.op(eng, lambda e: e.tensor_copy(out=out, in_=in_), reads, writes)

        pan_t = [TL("pan%d" % i) for i in range(NP)]
        conv_done = [0]

        def emit_conv(upto):
            while conv_done[0] < min(upto, NP):
                i = conv_done[0]
                pd = panels[i]
                kcp, pw = pd["kcp"], pd["pw"]
                src = pd["src"][pd["k0"] * 128:(pd["k0"] + kcp) * 128, pd["n0"]:pd["n0"] + pw]
                src = src.rearrange("(kc p) n -> p kc n", p=128)
                dst = wpan[i][:, 0:kcp * pw].rearrange("p (kc n) -> p kc n", kc=kcp)
                s_ = cv_sems[rr["cv"] % 8]
                rr["cv"] += 1
                P.dma("pool", dst, src, [], [pan_t[i]], s_)
                conv_done[0] += 1

        slot_gen = [0] * NSLOT

        def load_slot(src_ap, n, src_tiles):
            k = rr["slot"] % NSLOT
            rr["slot"] += 1
            dst = ring[:, k * SLOT:k * SLOT + n]
            P.dma("sp", dst, src_ap, src_tiles, [slot_t[k]], slot_sem[k])
            b = BK(dst, slot_t[k])
            b.k = k
            b.gen = rr["slot"]
            slot_gen[k] = b.gen
            return b

        def load_panel(key):
            i = pidx[key]
            emit_conv(i + 10)
            pd = panels[i]
            n = pd["kcp"] * pd["pw"]
            b = load_slot(wpan[i][:, 0:n], n, [pan_t[i]])
            b2 = BK(b.ap.rearrange("p (kc n) -> p kc n", kc=pd["kcp"]), b.t)
            b2.k = b.k
            b2.gen = b.gen
            return b2

        s0 = io_sem()
        for dst, src in ((gv, gv_d), (scw, scw_d), (fcw, fcw_d), (gsub, gsub_d), (cb, cb_d), (dl, dl_d)):
            P.dma("sp", dst[:], src, [], [cst_t], s0)
        P.dma("sp", ident32[:], ident_d, [], [ones_t], s0)
        P.op("dve", lambda e: e.memset(ones[:], 1.0), [], [ones_t])
        P.op("dve", lambda e: e.memset(warm[:], 1.0), [], [warm_t])
        P.op("dve", lambda e: e.tensor_copy(out=ident[:], in_=ident32[:]), [ones_t], [ones_t])
        P.op("dve", lambda e: e.memset(ones32[:], 1.0), [], [ones_t])
        P.op("dve", lambda e: e.memset(qz[0][:], 0.0), [], [qt_t])
        P.op("dve", lambda e: e.memset(qz[1][:], 0.0), [], [qt_t])
        P.op("dve", lambda e: e.memset(halo_sc[:], 0.0), [], [halo_sc_t])
        P.op("dve", lambda e: e.memset(halo_f[:], 0.0), [], [halo_f_t])
        tt("dve", lt[:, 0:64], dl[:, 0:64], dl[:, 64:128], ALU.mult, [cst_t], [cst_t])
        tt("dve", lt[:, 64:128], dl[:, 128:192], dl[:, 192:256], ALU.mult, [cst_t], [cst_t])
        P.op("dve", lambda e: e.reduce_sum(out=ls[:, 0:1], in_=lt[:, 0:64], axis=AX.X), [cst_t], [cst_t])
        P.op("dve", lambda e: e.reduce_sum(out=ls[:, 1:2], in_=lt[:, 64:128], axis=AX.X), [cst_t], [cst_t])
        act(ls[:, 2:4], ls[:, 0:2], AF.Exp, [cst_t], [cst_t])
        tt("dve", nlam[:, 0:1], ls[:, 3:4], ls[:, 2:3], ALU.subtract, [cst_t], [cst_t])
        P.op("dve", lambda e: e.tensor_scalar_add(out=nlam[:, 0:1], in0=nlam[:, 0:1], scalar1=-LI), [cst_t], [cst_t])
        P.op("dve", lambda e: e.tensor_scalar_mul(out=gsub[:, 0:1], in0=gsub[:, 0:1], scalar1=(1.0 - LI)), [cst_t], [cst_t])
        P.op("dve", lambda e: e.tensor_scalar_mul(out=ncb[:], in0=cb[:], scalar1=-1.0), [cst_t], [cst_t])
        expB_t = [TL("expB%d" % h) for h in range(NH)]
        P.dma("sp", xbuf[0][:], xT.rearrange("(c p) t -> p c t", p=128)[:, :, 0:T], [], x_t[0], io_sem())
        emit_conv(12)
        for h in range(NH):
            P.dma("pool", expBd[h].rearrange("p (o q) -> p o q", o=5), biasT[h].rearrange("p (o q) -> p o q", o=5),
                  [], [expB_t[h]], pio_sem())

        emit_conv(12)

        KT_dt = TL("KTd")
        V_dt = [TL("Vd%d" % s_) for s_ in range(4)]

        def sumsq_rstd(nchunks, dim):
            b = bank()
            for c in range(nchunks):
                mm(b.ap, ones[:], sq[:, c, :], c == 0, c == nchunks - 1, [ones_t, sq_t[c]], [b.t], signal=(c == nchunks - 1))
            rs = tmp()
            act(rs.ap[:, 0:T], b.ap, AF.Ln, [b.t], [rs.t], bias=EPS, scale=1.0 / dim)
            act(rs.ap[:, 0:T], rs.ap[:, 0:T], AF.Exp, [rs.t], [rs.t], scale=-0.5)
            return rs

        def prenorm(xi, outs, skip_square=False):
            x = xbuf[xi]
            if not skip_square:
                for half in range(2):
                    c0, c1 = 4 * half, 4 * half + 4
                    act(sq[:, c0:c1, :], x[:, c0:c1, :], AF.Square, x_t[xi][c0:c1], sq_t[c0:c1])
            rs = sumsq_rstd(KC, D)
            for gidx, ob, ot, eng in outs:
                for c in range(KC):
                    stt(eng, ob[:, c, :], x[:, c, :], gv[:, gidx * 8 + c:gidx * 8 + c + 1], rs.ap[:, 0:T],
                        ALU.mult, ALU.mult, [x_t[xi][c], rs.t, cst_t], [ot[c]])

        def postnorm_res(xi, gidx, sq_next=False):
            x = xbuf[xi]
            rs = stat_finish() if stat["bank"] is not None else sumsq_rstd(KC, D)
            for c in range(KC):
                tt("dve", ybuf[:, c, :], ybuf[:, c, :], rs.ap[:, 0:T], ALU.mult, [y_t[c], rs.t], [y_t[c]])
                stt("dve", x[:, c, :], ybuf[:, c, :], gv[:, gidx * 8 + c:gidx * 8 + c + 1], x[:, c, :],
                    ALU.mult, ALU.add, [y_t[c], x_t[xi][c], cst_t], [x_t[xi][c]])
                if sq_next:
                    act(sq[:, c, :], x[:, c, :], AF.Square, [x_t[xi][c]], [sq_t[c]])

        def evac_y(oc, b):
            act(sq[:, oc, :], b.ap, AF.Square, [], [sq_t[oc], b.t])
            cp("dve", ybuf[:, oc, :], b.ap, [b.t], [y_t[oc]])
            stat_push(oc)

        def proj8(pkeys, inb, in_t, evac, kouter=False):
            pn = None
            pre = {}
            if kouter:
                pn = load_panel(pkeys[0])
                bs_ = [bank() for _ in range(4)]
                for kc in range(KC):
                    for ocl in range(4):
                        mm(bs_[ocl].ap, pn.ap[:, kc, ocl * 128:(ocl + 1) * 128], inb[:, kc, :], kc == 0, kc == KC - 1,
                           [pn.t, in_t[kc]], [bs_[ocl].t], signal=(kc == KC - 1))
                for ocl in range(4):
                    pre[ocl] = bs_[ocl]
            for oc in range(8):
                if oc in pre:
                    stat_flush()
                    evac(oc, pre[oc])
                    continue
                if oc % 4 == 0:
                    pn = load_panel(pkeys[oc // 4])
                b = bank()
                for kc in range(KC):
                    mm(b.ap, pn.ap[:, kc, (oc % 4) * 128:(oc % 4 + 1) * 128], inb[:, kc, :], kc == 0, kc == KC - 1,
                       [pn.t, in_t[kc]], [b.t], signal=(kc == KC - 1))
                stat_flush()
                evac(oc, b)

        def conv_taps(src, acc, wt, wbase, wstride, reads_src):
            act(acc.ap[:, 0:T], src.ap[:, 2:T + 2], AF.Copy, [src.t, cst_t], [acc.t],
                scale=wt[:, wbase + 2 * wstride:wbase + 2 * wstride + 1])
            stt("dve", acc.ap[:, 0:T], src.ap[:, 1:T + 1], wt[:, wbase + wstride:wbase + wstride + 1], acc.ap[:, 0:T],
                ALU.mult, ALU.add, [src.t, acc.t, cst_t], [acc.t])
            stt("dve", acc.ap[:, 0:T], src.ap[:, 0:T], wt[:, wbase:wbase + 1], acc.ap[:, 0:T],
                ALU.mult, ALU.add, [src.t, acc.t, cst_t], [acc.t])

        def sc_mixer(xi):
            P.phase = "scmix"
            prenorm(xi, [(0, hb, hb_t, "dve")])
            pn = {}
            for c in range(KC):
                if c % 4 == 0:
                    for g_ in range(3):
                        pn[g_] = load_panel(("scin", 2 * g_ + c // 4))
                bs = []
                if c == 0:
                    bs = [bank() for _ in range(3)]
                    for kc in range(KC):
                        for g_ in range(3):
                            mm(bs[g_].ap, pn[g_].ap[:, kc, 0:128], hb[:, kc, :], kc == 0, kc == KC - 1,
                               [pn[g_].t, hb_t[kc]], [bs[g_].t], signal=(kc == KC - 1))
                else:
                    for g_ in range(3):
                        b = bank()
                        for kc in range(KC):
                            mm(b.ap, pn[g_].ap[:, kc, (c % 4) * 128:(c % 4 + 1) * 128], hb[:, kc, :], kc == 0, kc == KC - 1,
                               [pn[g_].t, hb_t[kc]], [b.t], signal=(kc == KC - 1))
                        bs.append(b)
                gcs = tmp()
                act(gcs.ap[:, 0:T], bs[1].ap, AF.Copy, [bs[1].t], [gcs.t])
                pr = tmp()
                cp("pool", pr.ap[:, 0:2], halo_sc[:, c, :], [halo_sc_t], [pr.t])
                tt("dve", pr.ap[:, 2:T + 2], gcs.ap[:, 0:T], bs[2].ap, ALU.mult, [gcs.t, bs[2].t], [pr.t])
                cp("pool", halo_sc[:, c, :], pr.ap[:, T:T + 2], [pr.t], [halo_sc_t])
                acc = tmp()
                conv_taps(pr, acc, scw, c, 8, None)
                tt("dve", hb2[:, c, :], acc.ap[:, 0:T], bs[0].ap, ALU.mult, [acc.t, bs[0].t], [hb2_t[c]])
            stat_begin()
            proj8([("scout", 0), ("scout", 1)], hb2, hb2_t, evac_y, kouter=True)
            postnorm_res(xi, 1, sq_next=True)

        def ffn(xi, l):
            P.phase = "ffn_up"
            prenorm(xi, [(6 * l + 2, hb, hb_t, "dve")], skip_square=True)
            pcur = {}
            pre = {}
            for cidx in (0, FC, 1, FC + 1):
                pnum = cidx // 4
                if pnum not in pcur:
                    pcur[pnum] = load_panel(("up", l, pnum))
                pre[cidx] = bank()
            for kc in range(KC):
                for cidx in (0, FC, 1, FC + 1):
                    pn = pcur[cidx // 4]
                    mm(pre[cidx].ap, pn.ap[:, kc, (cidx % 4) * 128:(cidx % 4 + 1) * 128], hb[:, kc, :], kc == 0, kc == KC - 1,
                       [pn.t, hb_t[kc]], [pre[cidx].t], signal=(kc == KC - 1))
            for j in range(FC):
                accs = []
                for which, cidx in ((0, j), (1, FC + j)):
                    if cidx in pre:
                        b = pre[cidx]
                    else:
                        pnum = cidx // 4
                        if pnum not in pcur or slot_gen[pcur[pnum].k] != pcur[pnum].gen:
                            pcur[pnum] = load_panel(("up", l, pnum))
                        pn = pcur[pnum]
                        b = bank()
                        for kc in range(KC):
                            mm(b.ap, pn.ap[:, kc, (cidx % 4) * 128:(cidx % 4 + 1) * 128], hb[:, kc, :], kc == 0, kc == KC - 1,
                               [pn.t, hb_t[kc]], [b.t], signal=(kc == KC - 1))
                    yc = tmp()
                    hidx = l * 44 + cidx
                    cp("pool", yc.ap[:, 0:2], halo_f[:, hidx, :], [halo_f_t], [yc.t])
                    act(yc.ap[:, 2:T + 2], b.ap, AF.Copy, [b.t], [yc.t])
                    cp("pool", halo_f[:, hidx, :], yc.ap[:, T:T + 2], [yc.t], [halo_f_t])
                    acc = tmp()
                    conv_taps(yc, acc, fcw, l * 132 + cidx, 44, None)
                    accs.append(acc)
                sg = tmp()
                act(sg.ap[:, 0:T], accs[0].ap[:, 0:T], AF.Silu, [accs[0].t], [sg.t])
                tt("dve", actb3[:, j, :], sg.ap[:, 0:T], accs[1].ap[:, 0:T], ALU.mult, [sg.t, accs[1].t], [act_t[j]])
                if j == FC - 1:
                    act(warm[:, 0:1], ones32[:, 0:1], AF.Ln, [ones_t], [warm_t])
            P.phase = "ffn_down"
            stat_begin()
            for nh in range(2):
                bs = [bank() for _ in range(4)]
                for kg, (k0, kcp) in enumerate(((0, 8), (8, 8), (16, 6))):
                    pn = load_panel(("down", l, nh, kg))
                    for ocl in range(4):
                        for kcl in range(kcp):
                            first = (kg == 0 and kcl == 0)
                            last = (kg == 2 and kcl == kcp - 1)
                            mm(bs[ocl].ap, pn.ap[:, kcl, ocl * 128:(ocl + 1) * 128], actb3[:, k0 + kcl, :], first, last,
                               [pn.t, act_t[k0 + kcl]], [bs[ocl].t], signal=(kcl == kcp - 1))
                        stat_flush()
                for ocl in range(4):
                    evac_y(nh * 4 + ocl, bs[ocl])
            postnorm_res(xi, 6 * l + 3, sq_next=True)

        def ple(xi, l, t0):
            P.phase = "ple"
            prenorm(xi, [(6 * l + 4, hb, hb_t, "dve")], skip_square=True)
            P.dma("sp", pst[:], pT[l].rearrange("(c p) t -> p c t", p=128)[:, :, t0:t0 + T], [], [pst_t], io_sem())
            act(pb[:], pst[:], AF.Copy, [pst_t], [pb_t])
            pp = load_panel(("pp", l))
            pg = None
            stat_begin()
            for oc in range(8):
                if oc % 4 == 0:
                    pg = load_panel(("pg", l, oc // 4))
                ba = bank()
                for kc in range(KC):
                    mm(ba.ap, pg.ap[:, kc, (oc % 4) * 128:(oc % 4 + 1) * 128], hb[:, kc, :], kc == 0, kc == KC - 1,
                       [pg.t, hb_t[kc]], [ba.t], signal=(kc == KC - 1))
                bb = bank()
                for kc in range(2):
                    mm(bb.ap, pp.ap[:, kc, oc * 128:(oc + 1) * 128], pb[:, kc, :], kc == 0, kc == 1,
                       [pp.t, pb_t], [bb.t], signal=(kc == 1))
                stat_flush()
                sg = tmp()
                act(sg.ap[:, 0:T], ba.ap, AF.Sigmoid, [ba.t], [sg.t])
                tt("dve", ybuf[:, oc, :], sg.ap[:, 0:T], bb.ap, ALU.mult, [sg.t, bb.t], [y_t[oc]])
                act(sq[:, oc, :], ybuf[:, oc, :], AF.Square, [y_t[oc]], [sq_t[oc]])
                stat_push(oc)
                if oc == 7:
                    act(warm[:, 0:1], ones32[:, 0:1], AF.Ln, [ones_t], [warm_t])
            postnorm_res(xi, 6 * l + 5, sq_next=(l == 0))

        def attention(xi, i, t0):
            P.phase = "kvq"
            prenorm(xi, [(12, hb2, hb2_t, "dve"), (6, hb, hb_t, "dve")], skip_square=True)
            kst = sq

            def evac_k(oc, b):
                act(kst[:, oc, :], b.ap, AF.Copy, [b.t], [sq_t[oc]])

            proj8([("kv", 0), ("kv", 1)], hb2, hb2_t, evac_k, kouter=True)
            P.dma("pool", KTd.rearrange("h p t -> p h t")[:, :, t0:t0 + T], kst[:], sq_t, [KT_dt], pio_sem())
            for nh in range(2):
                pn = load_panel(("kv", 2 + nh))
                for s_ in range(4):
                    b = bank()
                    for kc in range(KC):
                        mm(b.ap, hb2[:, kc, s_ * 128:(s_ + 1) * 128], pn.ap[:, kc, :], kc == 0, kc == KC - 1,
                           [pn.t, hb2_t[kc]], [b.t], signal=(kc == KC - 1))
                    if s_ % 2 == 0:
                        cp("dve", vst[:, s_, nh * 512:(nh + 1) * 512], b.ap, [b.t], vst_t)
                    else:
                        act(vst[:, s_, nh * 512:(nh + 1) * 512], b.ap, AF.Copy, [b.t], vst_t)
            for s_ in range(4):
                dstv = Vd.rearrange("h p (b e) -> p b h e", e=128)[:, 4 * i + s_, :, :]
                P.dma("pool", dstv, vst[:, s_, :].rearrange("p (h e) -> p h e", e=128), vst_t, [V_dt[s_]], pio_sem())

            def evac_q(oc, b):
                act(qz[0][0:64, oc, :], b.ap[0:64, :], AF.Copy, [b.t], [qt_t], scale=0.125)
                act(qz[1][64:128, oc, :], b.ap[64:128, :], AF.Copy, [b.t], [qt_t], scale=0.125)

            proj8([("q", 0), ("q", 1)], hb, hb_t, evac_q)

            nk = 4 * (i + 1)
            ao = hb2
            pending = []

            def run_pending(jnow):
                keep = []
                for trig, fn_ in pending:
                    if trig <= jnow:
                        fn_()
                    else:
                        keep.append((trig, fn_))
                pending[:] = keep

            for hh in range(NH):
                P.phase = "attn"
                ktp = load_slot(KTd[hh][:, 0:nk * 128], nk * 128, [KT_dt])
                vpb = load_slot(Vd[hh][:, 0:nk * 128], nk * 128, V_dt)
                vp = vpb.ap.rearrange("p (b e) -> p b e", e=128)
                ebb = load_slot(expBd[hh], 5 * T, [expB_t[hh]])
                eb = ebb.ap.rearrange("p (o q) -> p o q", o=5)
                par = hh % 2
                O = banks[4 + 2 * par:6 + 2 * par]
                acc = accD[par]

                D0 = banks[3]

                def emit_qk_m(j, m, ktp=ktp, hh=hh, ebb=ebb, eb=eb):
                    o_ = T * i - 128 * j
                    special = o_ <= 128
                    qs = max(0, -o_)
                    sb_ = banks[rr["s3"] % 3]
                    rr["s3"] += 1
                    mm(sb_.ap[:, qs:T], ktp.ap[:, j * 128:(j + 1) * 128], qz[m][:, hh, qs:T],
                       True, not special, [ktp.t, qt_t], [sb_.t], signal=(not special))
                    if special:
                        oi = (o_ + 384) // 128
                        mm(sb_.ap[:, qs:T], ident[:], eb[:, oi, qs:T], False, True, [ones_t, ebb.t], [sb_.t], signal=True)
                    return sb_

                Sb = [emit_qk_m(0, 0), emit_qk_m(0, 1)]
                for j in range(nk):
                    o_ = T * i - 128 * j
                    qs = max(0, -o_)
                    E = etile()
                    Sn = [None, None]
                    for m in range(2):
                        if j + 1 < nk:
                            Sn[m] = emit_qk_m(j + 1, m)
                        if o_ <= 128:
                            act(E.ap[:, m, qs:T], Sb[m].ap[:, qs:T], AF.Exp, [Sb[m].t], [E.tm[m]])
                        else:
                            act(E.ap[:, m, :], Sb[m].ap, AF.Exp, [Sb[m].t, cst_t], [E.tm[m]], bias=cb[:, hh:hh + 1])
                    for m in range(2):
                        mm(O[m].ap[:, qs:T], vp[:, j, :], E.ap[:, m, qs:T], j == 0, j == nk - 1, [vpb.t, E.tm[m]], [O[m].t],
                           signal=True)
                    mm(D0.ap[:, qs:T], ones[:], E.ap[:, 0, qs:T], j == 0, j == nk - 1, [ones_t, E.tm[0]], [D0.t], signal=True)
                    if j == 0:
                        cp("dve", acc.ap[:, 1, :], E.ap[:, 1, :], [E.tm[1]], [acc.t])
                    else:
                        tt("dve", acc.ap[:, 1, qs:T], acc.ap[:, 1, qs:T], E.ap[:, 1, qs:T], ALU.add, [acc.t, E.tm[1]], [acc.t])
                    Sb = Sn
                    run_pending(j)
                P.phase = "attn_fin"
                d1_ = banks[rr["s3"] % 3]
                rr["s3"] += 1
                mm(d1_.ap, ones32[:], acc.ap[:, 1, :], True, True, [ones_t, acc.t], [d1_.t], signal=True)
                Db = [D0, d1_]
                r0 = tmp()
                r1 = tmp()
                rs = tmp()
                sqo = sqos[par]
                for r_, d_ in ((r0, Db[0]), (r1, Db[1])):
                    act(r_.ap[:, 0:T], d_.ap, AF.Ln, [d_.t], [r_.t])
                    act(r_.ap[:, 0:T], r_.ap[:, 0:T], AF.Exp, [r_.t], [r_.t], scale=-1.0)

                def fin_a(r0=r0, r1=r1, O=O):
                    tt("dve", r0.ap[:, 0:T], O[0].ap, r0.ap[:, 0:T], ALU.mult, [O[0].t, r0.t], [r0.t])
                    tt("dve", r1.ap[:, 0:T], O[1].ap, r1.ap[:, 0:T], ALU.mult, [O[1].t, r1.t], [r1.t])
                    stt("dve", r0.ap[:, 0:T], r1.ap[:, 0:T], nlam[:, 0:1], r0.ap[:, 0:T], ALU.mult, ALU.add,
                        [r0.t, r1.t, cst_t], [r0.t])

                def fin_b(r0=r0, sqo=sqo, O=O):
                    act(sqo.ap, r0.ap[:, 0:T], AF.Square, [r0.t], [sqo.t])
                    mm(O[0].ap, ones[:], sqo.ap, True, True, [ones_t, sqo.t], [O[0].t], signal=True)

                def fin_c(rs=rs, O=O):
                    act(rs.ap[:, 0:T], O[0].ap, AF.Ln, [O[0].t], [rs.t], bias=EPS, scale=1.0 / 128)
                    act(rs.ap[:, 0:T], rs.ap[:, 0:T], AF.Exp, [rs.t], [rs.t], scale=-0.5)

                def fin_d(r0=r0, rs=rs, hh=hh):
                    stt("dve", ao[:, hh, :], r0.ap[:, 0:T], gsub[:, 0:1], rs.ap[:, 0:T], ALU.mult, ALU.mult,
                        [r0.t, rs.t, cst_t], [hb2_t[hh]])

                pending.append((0, fin_a))
                pending.append((min(2, nk - 1), fin_b))
                pending.append((min(4, nk - 1), fin_c))
                pending.append((min(6, nk - 1), fin_d))
            run_pending(10 ** 9)
            P.phase = "wo"
            stat_begin()
            proj8([("o", 0), ("o", 1)], ao, hb2_t, evac_y, kouter=True)
            postnorm_res(xi, 7, sq_next=True)

        for i in range(n_tiles):
            xi = i % 2
            t0 = i * T
            if i > 0:
                P.dma("sp", xbuf[xi][:], xT.rearrange("(c p) t -> p c t", p=128)[:, :, t0:t0 + T], [], x_t[xi], io_sem())
            sc_mixer(xi)
            if stage >= 2:
                ffn(xi, 0)
            if stage >= 3:
                ple(xi, 0, t0)
            if stage >= 4:
                attention(xi, i, t0)
            if stage >= 5:
                ffn(xi, 1)
            if stage >= 6:
                ple(xi, 1, t0)
            P.dma("pool", outT.rearrange("(c p) t -> p c t", p=128)[:, :, t0:t0 + T], xbuf[xi][:], x_t[xi], [], out_sems[xi])
        P.final_wait("pool", out_sems)

        with nc.Block() as block:
            @block.tensor
            def _(e):
                P.replay("pe", e)

            @block.scalar
            def _(e):
                P.replay("act", e)

            @block.vector
            def _(e):
                P.replay("dve", e)

            @block.gpsimd
            def _(e):
                P.replay("pool", e)

            @block.sync
            def _(e):
                P.replay("sp", e)
    build_program.last_prog = P
    return nc


def _rel_bucket_np(dist):
    max_exact = 16
    d = np.maximum(dist, 1).astype(np.float32)
    large = max_exact + (np.log(d / np.float32(max_exact)) / np.float32(math.log(128 / max_exact))
                         * np.float32(32 - max_exact)).astype(np.int32)
    large = np.minimum(large, 31)
    return np.where(dist < max_exact, dist, large)


def _host_inputs(inp, b):
    f = np.float32
    x = np.asarray(inp["x"], f)
    p = np.asarray(inp["p"], f)
    m = {}
    m["xT"] = np.ascontiguousarray(x[b].T)
    m["pT"] = np.ascontiguousarray(p[:, b].transpose(0, 2, 1))
    gl = []
    for l in range(2):
        for nm in ("g_pre_mix", "g_post_mix", "g_pre_ffn", "g_post_ffn", "g_pre_ple", "g_post_ple"):
            gl.append(np.asarray(inp[nm], f)[l])
    gl.append(np.asarray(inp["g_kv"], f))
    g = np.stack(gl, 0).reshape(13, 8, 128).transpose(2, 0, 1).reshape(128, 104)
    m["gv"] = np.ascontiguousarray(g)
    scw = np.asarray(inp["w_sc_conv"], f)[0].reshape(3, 8, 128).transpose(2, 0, 1).reshape(128, 24)
    m["scw"] = np.ascontiguousarray(scw)
    fcw = np.asarray(inp["w_ffn_conv"], f).reshape(2, 3, 44, 128).transpose(3, 0, 1, 2).reshape(128, 264)
    m["fcw"] = np.ascontiguousarray(fcw)
    m["gsub"] = np.ascontiguousarray(np.asarray(inp["g_subln"], f)[0].reshape(128, 1))
    rb = np.asarray(inp["rel_bias"], f)
    m["cb"] = np.ascontiguousarray(np.broadcast_to(rb[31:32, :], (128, NH)))
    m["dl"] = np.ascontiguousarray(np.broadcast_to(np.asarray(inp["diff_lambda"], f)[0].reshape(1, 256), (128, 256)))
    kk = np.arange(128)[:, None]
    qq = np.arange(T)[None, :]
    bt = np.empty((NH, 128, 5, T), f)
    for oi in range(5):
        o_ = -384 + 128 * oi
        d = o_ + qq - kk
        bk = _rel_bucket_np(np.maximum(d, 0))
        vals = rb[bk]
        vals = np.where((d >= 0)[:, :, None], vals, f(NEG))
        bt[:, :, oi, :] = vals.transpose(2, 0, 1)
    m["biasT"] = np.ascontiguousarray(bt.reshape(NH, 128, 5 * T))
    m["ident"] = np.eye(128, dtype=f)
    m["w_sc_in"] = np.ascontiguousarray(np.asarray(inp["w_sc_in"], f)[0])
    m["w_sc_out"] = np.ascontiguousarray(np.asarray(inp["w_sc_out"], f)[0])
    m["w_kv"] = np.ascontiguousarray(np.asarray(inp["w_kv"], f))
    m["w_q"] = np.ascontiguousarray(np.asarray(inp["w_q"], f)[0])
    m["w_o"] = np.ascontiguousarray(np.asarray(inp["w_o"], f)[0])
    m["w_up"] = np.ascontiguousarray(np.asarray(inp["w_ffn_up"], f))
    m["w_down"] = np.ascontiguousarray(np.asarray(inp["w_ffn_down"], f))
    m["w_pg"] = np.ascontiguousarray(np.asarray(inp["w_ple_gate"], f))
    m["w_pp"] = np.ascontiguousarray(np.asarray(inp["w_ple_proj"], f))
    return m


def kernel(**inputs):
    nc = build_program()
    shared = None
    in_maps = []
    for b in range(8):
        m = _host_inputs(inputs, b) if shared is None else None
        if shared is None:
            shared = m
        else:
            f = np.float32
            m = dict(shared)
            m["xT"] = np.ascontiguousarray(np.asarray(inputs["x"], f)[b].T)
            m["pT"] = np.ascontiguousarray(np.asarray(inputs["p"], f)[:, b].transpose(0, 2, 1))
        in_maps.append(m)
    res = run_bass_kernel_spmd(nc, in_maps, core_ids=list(range(8)))
    out = np.stack([np.asarray(r["outT"]).T for r in res.results], 0)
    return np.ascontiguousarray(out.astype(np.float32))
```
